# Optimizing a Trainium2 kernel written in Bass

```python
import math
import jax, jax.numpy as jnp
from jax import lax
import numpy as np

D_MODEL = 1024
BATCH = 32
SEQ = 2048
DEPTH = 4

N_MIXERS = 2
N_S5_LAYERS = (DEPTH + 1) // 2
N_POOL_LAYERS = DEPTH // 2
S5_GROUP_CH = 16
S5_GROUPS = D_MODEL // S5_GROUP_CH
S5_STATE = 64
S5_CHUNK = 128
DT_MIN = 1e-3
DT_MAX = 1e-1
POOL_WINDOWS = (2, 4, 8, 16)
POOL_GROUPS = len(POOL_WINDOWS)
POOL_GROUP_CH = D_MODEL // POOL_GROUPS
D_FF = 2816
CONV_WIDTH = 3
RMS_EPS = 1e-6

kernel_name = "hybrid_s5_pool_convffn"


def rms_norm(x, gain):
    xf = x.astype(jnp.float32)
    y = xf * lax.rsqrt(jnp.mean(xf * xf, axis=-1, keepdims=True) + RMS_EPS)
    return (y * gain.astype(jnp.float32)).astype(x.dtype)


def _complex_affine_combine(e1, e2):
    a1r, a1i, b1r, b1i = e1
    a2r, a2i, b2r, b2i = e2
    ar = a2r * a1r - a2i * a1i
    ai = a2r * a1i + a2i * a1r
    br = a2r * b1r - a2i * b1i + b2r
    bi = a2r * b1i + a2i * b1r + b2i
    return ar, ai, br, bi


def s5_mixer(u, lam_re, lam_im, log_dt, b_re, b_im, c_re, c_im, d_skip, w_glu, b_glu):
    f32 = jnp.float32
    bsz, l, d = u.shape
    lr, li = lam_re.astype(f32), lam_im.astype(f32)
    dt = jnp.exp(log_dt.astype(f32))[:, None]
    mag = jnp.exp(lr * dt)
    ab_re = mag * jnp.cos(li * dt)
    ab_im = mag * jnp.sin(li * dt)
    den = lr * lr + li * li
    nr = ab_re - 1.0
    ni = ab_im
    f_re = ((nr * lr + ni * li) / den)[..., None]
    f_im = ((ni * lr - nr * li) / den)[..., None]
    br, bi = b_re.astype(f32), b_im.astype(f32)
    bb_re = f_re * br - f_im * bi
    bb_im = f_re * bi + f_im * br
    cr, ci = c_re.astype(f32), c_im.astype(f32)

    n_chunks = l // S5_CHUNK
    uf = u.astype(f32)
    u_chunks = uf.reshape(bsz, n_chunks, S5_CHUNK, S5_GROUPS, S5_GROUP_CH).transpose(1, 0, 2, 3, 4)
    a_re = jnp.broadcast_to(ab_re, (1, S5_CHUNK, S5_GROUPS, S5_STATE))
    a_im = jnp.broadcast_to(ab_im, (1, S5_CHUNK, S5_GROUPS, S5_STATE))

    def step(carry, u_c):
        h0r, h0i = carry
        xr = jnp.einsum('bcgh,gph->bcgp', u_c, bb_re)
        xi = jnp.einsum('bcgh,gph->bcgp', u_c, bb_im)
        pr, pi_, sr, si = lax.associative_scan(_complex_affine_combine, (a_re, a_im, xr, xi), axis=1)
        h0r_, h0i_ = h0r[:, None], h0i[:, None]
        hr = sr + pr * h0r_ - pi_ * h0i_
        hi = si + pr * h0i_ + pi_ * h0r_
        y = jnp.einsum('bcgp,ghp->bcgh', hr, cr) - jnp.einsum('bcgp,ghp->bcgh', hi, ci)
        return (hr[:, -1], hi[:, -1]), y

    init = (jnp.zeros((bsz, S5_GROUPS, S5_STATE), f32), jnp.zeros((bsz, S5_GROUPS, S5_STATE), f32))
    _, ys = lax.scan(step, init, u_chunks)
    y = ys.transpose(1, 0, 2, 3, 4).reshape(bsz, l, d) + d_skip.astype(f32) * uf
    y = jax.nn.gelu(y)
    out = y * jax.nn.sigmoid(y @ w_glu.astype(f32) + b_glu.astype(f32))
    return out.astype(u.dtype)


def pool_mixer(u, w_group, scale):
    f32 = jnp.float32
    bsz, l, d = u.shape
    uf = u.astype(f32)
    cs = jnp.concatenate([jnp.zeros((bsz, 1, d), f32), jnp.cumsum(uf, axis=1)], axis=1)
    t = jnp.arange(1, l + 1, dtype=f32)
    pooled = []
    for g, win in enumerate(POOL_WINDOWS):
        csg = cs[:, :, g * POOL_GROUP_CH:(g + 1) * POOL_GROUP_CH]
        upper = csg[:, 1:]
        lower = jnp.pad(csg, ((0, 0), (win - 1, 0), (0, 0)))[:, :l]
        count = jnp.minimum(t, float(win))[None, :, None]
        pooled.append((upper - lower) / count)
    pooled = jnp.stack(pooled, axis=2)
    diff = pooled - uf.reshape(bsz, l, POOL_GROUPS, POOL_GROUP_CH)
    out = jnp.einsum('bsgc,gcd->bsgd', diff, w_group.astype(f32)).reshape(bsz, l, d)
    return (out * scale.astype(f32)).astype(u.dtype)


def conv_ffn(u, w_gate, w_val, conv_w, conv_b, w_down):
    l = u.shape[1]
    g = u @ w_gate
    gp = jnp.pad(g, ((0, 0), (CONV_WIDTH - 1, 0), (0, 0)))
    gc = conv_b + conv_w[0] * gp[:, 0:l]
    for k in range(1, CONV_WIDTH):
        gc = gc + conv_w[k] * gp[:, k:k + l]
    hdn = jax.nn.gelu(gc) * (u @ w_val)
    return hdn @ w_down


def setup_inputs(seed: int = 0) -> dict:
    key = jax.random.key(seed)
    ks = jax.random.split(key, 22)
    f32 = jnp.float32
    na, nb = N_S5_LAYERS, N_POOL_LAYERS
    G, P, H, D, F = S5_GROUPS, S5_STATE, S5_GROUP_CH, D_MODEL, D_FF
    nrm = lambda k, s: jax.random.normal(k, s, f32)
    n_idx = jnp.arange(P, dtype=f32)
    return {
        "x": nrm(ks[0], (BATCH, SEQ, D)),
        "s5_lambda_re": -0.5 + 0.01 * nrm(ks[1], (na, G, P)),
        "s5_lambda_im": math.pi * n_idx + 0.01 * nrm(ks[2], (na, G, P)),
        "s5_log_dt": jax.random.uniform(ks[3], (na, G), f32, math.log(DT_MIN), math.log(DT_MAX)),
        "s5_b_re": nrm(ks[4], (na, G, P, H)) * (2 * H) ** -0.5,
        "s5_b_im": nrm(ks[5], (na, G, P, H)) * (2 * H) ** -0.5,
        "s5_c_re": nrm(ks[6], (na, G, H, P)) * P ** -0.5,
        "s5_c_im": nrm(ks[7], (na, G, H, P)) * P ** -0.5,
        "s5_d": nrm(ks[8], (na, D)),
        "s5_w_glu": nrm(ks[9], (na, D, D)) * D ** -0.5,
        "s5_b_glu": 0.01 * nrm(ks[10], (na, D)),
        "pool_w": nrm(ks[11], (nb, POOL_GROUPS, POOL_GROUP_CH, POOL_GROUP_CH)) * POOL_GROUP_CH ** -0.5,
        "pool_scale": 1.0 + 0.1 * nrm(ks[12], (nb, D)),
        "ffn_w_gate": nrm(ks[13], (DEPTH, D, F)) * D ** -0.5,
        "ffn_w_val": nrm(ks[14], (DEPTH, D, F)) * D ** -0.5,
        "ffn_conv_w": nrm(ks[15], (DEPTH, CONV_WIDTH, F)) * CONV_WIDTH ** -0.5,
        "ffn_conv_b": 0.01 * nrm(ks[16], (DEPTH, F)),
        "ffn_w_down": nrm(ks[17], (DEPTH, F, D)) * F ** -0.5,
        "norm_mix_pre": 1.0 + 0.05 * nrm(ks[18], (DEPTH, D)),
        "norm_mix_post": 1.0 + 0.05 * nrm(ks[19], (DEPTH, D)),
        "norm_ffn_pre": 1.0 + 0.05 * nrm(ks[20], (DEPTH, D)),
        "norm_ffn_post": 1.0 + 0.05 * nrm(ks[21], (DEPTH, D)),
    }


def reference(x, s5_lambda_re, s5_lambda_im, s5_log_dt, s5_b_re, s5_b_im, s5_c_re, s5_c_im,
              s5_d, s5_w_glu, s5_b_glu, pool_w, pool_scale, ffn_w_gate, ffn_w_val, ffn_conv_w,
              ffn_conv_b, ffn_w_down, norm_mix_pre, norm_mix_post, norm_ffn_pre, norm_ffn_post):
    for i in range(DEPTH):
        j = i // N_MIXERS
        h = rms_norm(x, norm_mix_pre[i])
        if i % N_MIXERS == 0:
            m = s5_mixer(h, s5_lambda_re[j], s5_lambda_im[j], s5_log_dt[j], s5_b_re[j], s5_b_im[j],
                         s5_c_re[j], s5_c_im[j], s5_d[j], s5_w_glu[j], s5_b_glu[j])
        else:
            m = pool_mixer(h, pool_w[j], pool_scale[j])
        x = x + rms_norm(m, norm_mix_post[i])
        h = rms_norm(x, norm_ffn_pre[i])
        f = conv_ffn(h, ffn_w_gate[i], ffn_w_val[i], ffn_conv_w[i], ffn_conv_b[i], ffn_w_down[i])
        x = x + rms_norm(f, norm_ffn_post[i])
    return x
```

```python
import contextlib
import numpy as np
import concourse.bass as bass
import concourse.mybir as mybir
from concourse.bass_utils import run_bass_kernel_spmd

F32 = mybir.dt.float32
BF16 = mybir.dt.bfloat16
I32 = mybir.dt.int32
AF = mybir.ActivationFunctionType
ALU = mybir.AluOpType

D = 1024
L_SEQ = 2048
DFF = 2816
NF = DFF // 128
NT = L_SEQ // 128
NTB = 4
DEPTH = 4
NCORES = 8
EPS = 1e-6
ENGS = ["sync", "gpsimd", "scalar", "vector", "tensor"]
DEBUG = False
REPEAT_FIRST = False


class Prog:
    def __init__(self):
        self.ops = []
        self.lastw = {}
        self.readers = {}
        self.dma_cnt = {}

    def op(self, eng, fn, r=(), w=(), dma=None, strict=False):
        i = len(self.ops)
        deps = set()
        for k in r:
            if k in self.lastw:
                deps.add(self.lastw[k])
        for k in w:
            if k in self.lastw:
                deps.add(self.lastw[k])
            deps.update(self.readers.get(k, ()))
        for k in w:
            self.lastw[k] = i
            self.readers[k] = []
        for k in r:
            self.readers.setdefault(k, []).append(i)
        dval = None
        if dma is not None:
            self.dma_cnt[dma] = self.dma_cnt.get(dma, 0) + 16
            dval = self.dma_cnt[dma]
        deps.discard(i)
        self.ops.append(dict(eng=eng, fn=fn, deps=deps, dma=dma, dval=dval, inc=False, tick=None, strict=strict))
        return i

    def emit(self, nc, stack):
        ops = self.ops
        for o in ops:
            best = {}
            keep = []
            for d in o["deps"]:
                p = ops[d]
                if p["dma"] is not None:
                    keep.append(d)
                elif p["eng"] == o["eng"] and o["dma"] is None and not o["strict"]:
                    continue
                elif p["eng"] == o["eng"] and o["dma"] is not None and not o["strict"]:
                    continue
                else:
                    if p["eng"] not in best or best[p["eng"]] < d:
                        best[p["eng"]] = d
            dmax = {}
            for d in keep:
                kk = ops[d]["dma"]
                if kk not in dmax or ops[dmax[kk]]["dval"] < ops[d]["dval"]:
                    dmax[kk] = d
            keep = list(dmax.values())
            keep.extend(best.values())
            o["deps"] = sorted(keep)
            for d in best.values():
                ops[d]["inc"] = True
        cnt = {e: 0 for e in ENGS}
        for o in ops:
            if o["dma"] is None and o["inc"]:
                cnt[o["eng"]] += 1
                o["tick"] = cnt[o["eng"]]
        sems = {e: stack.enter_context(nc.semaphore("s_" + e)) for e in ENGS}
        dsems = {}
        for k in self.dma_cnt:
            dsems[k] = stack.enter_context(nc.semaphore("d_" + "_".join(str(x) for x in (k if isinstance(k, tuple) else (k,)))))
        block = stack.enter_context(nc.Block())

        def run(eng_name):
            def body(e):
                waited = {}
                for o in ops:
                    if o["eng"] != eng_name:
                        continue
                    for d in o["deps"]:
                        p = ops[d]
                        if p["dma"] is not None:
                            key, val = ("d", p["dma"]), p["dval"]
                            sem = dsems[p["dma"]]
                        else:
                            key, val = ("e", p["eng"]), p["tick"]
                            sem = sems[p["eng"]]
                        if waited.get(key, 0) >= val:
                            continue
                        waited[key] = val
                        e.wait_ge(sem, val)
                    if o["fn"] is None:
                        continue
                    ins = o["fn"](e)
                    if o["dma"] is not None:
                        ins.then_inc(dsems[o["dma"]], 16)
                    elif o["inc"]:
                        ins.then_inc(sems[eng_name], 1)
            return body

        block.sync(run("sync"))
        block.gpsimd(run("gpsimd"))
        block.scalar(run("scalar"))
        block.vector(run("vector"))
        block.tensor(run("tensor"))


def build(nseq=4, layers=(0, 1, 2, 3), do_mixer=True, do_ffn=True, scan_eng="vector", pool_eng="gpsimd"):
    nc = bass.Bass("TRN2", target_bir_lowering=False)
    nl = len(layers)
    x_d = nc.dram_tensor("x", [nseq, L_SEQ, D], F32, kind="ExternalInput").ap()
    y_d = nc.dram_tensor("y", [nseq, L_SEQ, D], F32, kind="ExternalOutput").ap()
    wffn_d = nc.dram_tensor("wffn", [DEPTH, NF, 128, 3072], F32, kind="ExternalInput").ap()
    wffn_b = nc.dram_tensor("wffn_b", [DEPTH, NF, 128, 3072], BF16, kind="Internal").ap()
    PP_L = 8 + 8 + 66 + 22
    PP_S5 = DEPTH * PP_L
    PP_W = PP_S5 + 32
    pp_d = nc.dram_tensor("pp", [128, PP_W], F32, kind="ExternalInput").ap()
    gpost_d = nc.dram_tensor("gpost", [DEPTH, 2, D], F32, kind="ExternalInput").ap()
    ident_d = nc.dram_tensor("ident", [128, 128], F32, kind="ExternalInput").ap()
    mask_d = nc.dram_tensor("mask", [128, 128], F32, kind="ExternalInput").ap()
    perm_d = nc.dram_tensor("perm", [128, 128], F32, kind="ExternalInput").ap()
    band_d = nc.dram_tensor("band", [128, 12 * 128], F32, kind="ExternalInput").ap()
    poolw_d = nc.dram_tensor("poolw", [2, 128, 2048], F32, kind="ExternalInput").ap()
    pscale_d = nc.dram_tensor("pscale", [2, D], F32, kind="ExternalInput").ap()
    poolw_b = nc.dram_tensor("poolw_b", [2, 128, 2048], BF16, kind="Internal").ap()
    s5p_d = nc.dram_tensor("s5p", [2, 128, 96], F32, kind="ExternalInput").ap()
    s5bc_d = nc.dram_tensor("s5bc", [2, 4, 128, 1024], F32, kind="ExternalInput").ap()
    wglu_d = nc.dram_tensor("wglu", [2, 128, 8192], F32, kind="ExternalInput").ap()
    wglu_b = nc.dram_tensor("wglu_b", [2, 128, 8192], BF16, kind="Internal").ap()
    s5w_b = nc.dram_tensor("s5w_b", [2, 8, 128, 5120], BF16, kind="Internal").ap()

    if DEBUG:
        dbg_d = nc.dram_tensor("dbg", [128, 8, 2048], BF16, kind="ExternalOutput").ap()
        dbg_w = nc.dram_tensor("dbg_w", [8, 128, 5120], BF16, kind="ExternalOutput").ap()
        dbg_sc = nc.dram_tensor("dbg_sc", [128, 256], F32, kind="ExternalOutput").ap()
        dbg_z = nc.dram_tensor("dbg_z", [128, 64 * 256], BF16, kind="ExternalOutput").ap()
        dbg_s = nc.dram_tensor("dbg_s", [128, 64 * 256], BF16, kind="ExternalOutput").ap()
        dbg_h = nc.dram_tensor("dbg_h", [128, 8 * (L_SEQ + 16)], BF16, kind="ExternalOutput").ap()
        dbg_u = nc.dram_tensor("dbg_u", [128, 8 * (L_SEQ + 16)], BF16, kind="ExternalOutput").ap()
    P = Prog()
    st = contextlib.ExitStack()
    with st:
        def sb(name, shape, dt):
            return st.enter_context(nc.sbuf_tensor(name, shape, dt))

        HW = L_SEQ + 16
        x_sb = sb("x_sb", [128, NT, D], F32)
        h_sb = sb("h_sb", [128, 8, HW], BF16)
        ARENA = 80 * 1024
        arena = sb("arena", [128, ARENA // 2], BF16)

        def carve(off, shape, dt):
            n = int(np.prod(shape))
            nb = n * (2 if dt == BF16 else 4)
            a = arena[:, off // 2:(off + nb) // 2]
            if dt != BF16:
                a = a.bitcast(dt)
            if len(shape) == 2:
                a = a.rearrange("p (a b) -> p a b", a=shape[0])
            elif len(shape) == 3:
                a = a.rearrange("p (a b c) -> p a b c", a=shape[0], b=shape[1])
            return a

        def mkap(base, add_off, dims):
            return bass.AP(base.tensor, base.offset + add_off, [list(base.ap[0])] + [list(d) for d in dims])

        halo_sb = sb("halo_sb", [128, NF, 2], F32)
        t0_sb = sb("t0_sb", [128, 2, 512], F32)
        junk_sb = sb("junk_sb", [128, D], BF16)
        hb_sb = sb("hb_sb", [128, 3, D], BF16)
        tmp_sb = sb("tmp_sb", [128, 1, D], F32)
        ss_sb = sb("ss_sb", [128, 64], F32)
        rs_sb = sb("rs_sb", [128, 64], F32)
        pp_sb = sb("pp_sb", [128, PP_W], F32)
        gpost_sb = sb("gpost_sb", [128, 2, D], F32)
        cst_sb = sb("cst_sb", [128, 4], F32)
        ident_f = sb("ident_f", [128, 128], F32)
        ident_b = sb("ident_b", [128, 128], BF16)
        mask_sb = sb("mask_sb", [128, 128], F32)
        perm_b = sb("perm_b", [128, 128], BF16)
        scoef_sb = sb("scoef_sb", [128, 2, 2, 64], F32)
        ps = st.enter_context(nc.psum_tensor("ps", [128, 8, 512], F32))

        def OPA(eng, fn, r=(), w=(), dma=None, strict=False):
            return P.op(eng, fn, list(r) + ["arena"], w, dma, strict)

        def phase_enter():
            P.op("vector", lambda e: e.memset(cst_sb[:, 1:2], 0.0), w=["arena"])

        P.op("sync", lambda e: e.dma_start(out=pp_sb[:], in_=pp_d), w=["pp"], dma="c_pp")
        P.op("sync", lambda e: e.dma_start(out=ident_f[:], in_=ident_d), w=["ident_f"], dma="c_id")
        P.op("sync", lambda e: e.dma_start(out=mask_sb[:], in_=mask_d), w=["mask"], dma="c_mk")
        P.op("gpsimd", lambda e: e.dma_start(out=perm_b[:], in_=perm_d), w=["perm_b"], dma="c_pm")
        P.op("vector", lambda e: e.tensor_copy(out=ident_b[:], in_=ident_f[:]), r=["ident_f"], w=["ident_b"])
        P.op("vector", lambda e: e.memset(cst_sb[:, 0:1], -0.5), w=["cst"])
        P.op("vector", lambda e: e.memset(h_sb[:, :, 0:16], 0.0), w=["hpad"])

        def cast_layer(l):
            for f in range(NF):
                P.op("gpsimd", lambda e, l=l, f=f: e.dma_start(out=wffn_b[l, f], in_=wffn_d[l, f]),
                     w=[("wffn_b", l, f)], dma=("cast", l))

        def pp(l, off, n=1):
            return pp_sb[:, l * PP_L + off: l * PP_L + off + n]

        cnt = {"ss": 0, "gv": 0, "t": 0, "hb": 0, "dn": 0, "bund": 0, "za": 0, "y": 0, "glu": 0, "sg": 0, "mf": 0, "pk": 0}
        HK = [("hk", k) for k in range(8)]
        HT = [("ht", t) for t in range(NT)]

        def rms_stats(src_ap, key_r, width=D):
            c = cnt["ss"] % 64
            cnt["ss"] += 1
            P.op("scalar", lambda e: e.activation(out=junk_sb[:, 0:width], in_=src_ap, func=AF.Square,
                                                   accum_out=ss_sb[:, c:c + 1]),
                 r=key_r, w=["junk", ("ss", c)])
            P.op("vector", lambda e: e.tensor_scalar(out=ss_sb[:, c:c + 1], in0=ss_sb[:, c:c + 1], scalar1=1.0 / width,
                                                      scalar2=EPS, op0=ALU.mult, op1=ALU.add),
                 r=[("ss", c)], w=[("ss", c)])
            P.op("gpsimd", lambda e: e.tensor_tensor(out=rs_sb[:, c:c + 1], in0=ss_sb[:, c:c + 1], in1=cst_sb[:, 0:1],
                                                      op=ALU.pow),
                 r=[("ss", c), "cst"], w=[("rs", c)])
            return rs_sb[:, c:c + 1], ("rs", c)

        def prenorm_stats(t):
            return rms_stats(x_sb[:, t, :], [("x", t)])

        def prenorm_apply(l, t, goff, mode, stat, band=None, first=True):
            parts = prenorm_parts(l, t, goff, mode, stat, band=band, first=first)
            parts["copy"]()
            parts["pe"]()
            parts["evac"]()

        def prenorm_parts(l, t, goff, mode, stat, band=None, first=True):
            rstd, rkey = stat
            hs = cnt["hb"] % 3
            hprev = (cnt["hb"] - 1) % 3
            cnt["hb"] += 1
            b0 = 4 + (cnt["hb"] % 2) * 2
            gain = pp(l, goff, 8)

            def copy():
                P.op("scalar", lambda e: e.activation(out=hb_sb[:, hs, :], in_=x_sb[:, t, :], func=AF.Copy, scale=rstd),
                     r=[("x", t), rkey], w=[("hb", hs)])

            if mode == "nat":
                pt = ps[:, b0, :].bitcast(BF16).rearrange("p (k t) -> p k t", k=8)

                def pe():
                    for k in range(8):
                        P.op("tensor", lambda e, k=k: e.transpose(out=pt[:, k, :], in_=hb_sb[:, hs, k * 128:(k + 1) * 128],
                                                                   identity=ident_b[:]),
                             r=[("hb", hs), "ident_b"], w=[("ps", b0)])

                def evac():
                    out = h_sb[:, :, 16 + t * 128:16 + (t + 1) * 128]
                    P.op("vector", lambda e: e.tensor_tensor(out=out, in0=pt, in1=gain.unsqueeze(2).to_broadcast([128, 8, 128]),
                                                              op=ALU.mult),
                         r=[("ps", b0), "pp"], w=[("ht", t)])
                return dict(copy=copy, pe=pe, evac=evac)

            pt = ps[:, b0:b0 + 2, :].rearrange("p b (k t) -> p (b k) t", k=4)
            pkeys = [("ps", b0), ("ps", b0 + 1)]
            if mode == "deint":
                def pe():
                    for k in range(8):
                        P.op("tensor", lambda e, k=k: e.matmul(pt[:, k, :], hb_sb[:, hs, k * 128:(k + 1) * 128], perm_b[:],
                                                                start=True, stop=True),
                             r=[("hb", hs), "perm_b"], w=[("ps", b0 + k // 4)])

                def evac():
                    out = mkap(h_sb[:], 16 + t * 16, [[HW, 8], [256, 8], [1, 16]])
                    in0 = pt.rearrange("p k (s c) -> p k s c", s=8)
                    g4 = mkap(gain, 0, [[1, 8], [0, 8], [0, 16]])
                    P.op("vector", lambda e: e.tensor_tensor(out=out, in0=in0, in1=g4, op=ALU.mult),
                         r=pkeys + ["pp"], w=HT + HK)
                return dict(copy=copy, pe=pe, evac=evac)

            def pe():
                for k in range(8):
                    g = k // 2
                    bm = band[:, (8 + g if first else g), :]
                    P.op("tensor", lambda e, k=k, bm=bm: e.matmul(pt[:, k, :], hb_sb[:, hs, k * 128:(k + 1) * 128], bm,
                                                                   start=True, stop=first),
                         r=[("hb", hs), "band"], w=[("ps", b0 + k // 4)])
                    if not first:
                        bp = band[:, 4 + g, :]
                        P.op("tensor", lambda e, k=k, bp=bp: e.matmul(pt[:, k, :], hb_sb[:, hprev, k * 128:(k + 1) * 128], bp,
                                                                       start=False, stop=True),
                             r=[("hb", hprev), "band"], w=[("ps", b0 + k // 4)])

            def evac():
                out = h_sb[:, :, 16 + t * 128:16 + (t + 1) * 128]
                if t % 2 == 0:
                    P.op("vector", lambda e: e.tensor_copy(out=out, in_=pt), r=pkeys, w=[("ht", t)])
                else:
                    P.op("scalar", lambda e: e.activation(out=out, in_=pt, func=AF.Copy), r=pkeys, w=[("ht", t)])
            return dict(copy=copy, pe=pe, evac=evac)

        def prenorm_many(l, tiles, goff, mode, band=None, early=None):
            early = early or {}
            gs_ = lambda t: early.pop(t) if t in early else prenorm_stats(t)
            st_next = gs_(tiles[0])
            for i, t in enumerate(tiles):
                st_cur = st_next
                if i + 1 < len(tiles):
                    st_next = gs_(tiles[i + 1])
                prenorm_apply(l, t, goff, mode, st_cur, band=band, first=(t == 0))

        def post_stats(src_ap, src_keys):
            return rms_stats(src_ap, src_keys)

        def post_apply(which, src_ap, src_keys, t, stat):
            rstd, rkey = stat
            P.op("vector", lambda e: e.tensor_tensor(out=tmp_sb[:, 0, :], in0=src_ap, in1=gpost_sb[:, which, :], op=ALU.mult),
                 r=list(src_keys) + [("gpost", which)], w=["tmp"])
            P.op("vector", lambda e: e.scalar_tensor_tensor(out=x_sb[:, t, :], in0=tmp_sb[:, 0, :], scalar=rstd,
                                                             in1=x_sb[:, t, :], op0=ALU.mult, op1=ALU.add),
                 r=["tmp", rkey, ("x", t)], w=[("x", t)])

        def post_update(l, which, src_ap, src_keys, t):
            post_apply(which, src_ap, src_keys, t, post_stats(src_ap, src_keys))

        def load_gpost(l, which):
            P.op("sync", lambda e: e.dma_start(out=gpost_sb[:, which, :], in_=gpost_d[l, which:which + 1, :].partition_broadcast(128)),
                 w=[("gpost", which)], dma=("gp", which))

        NSLOT = 3
        wd_sb = carve(0, [NF, D], BF16)
        wgv_sb = carve(45056, [NSLOT, 2048], BF16)
        hdn_sb = carve(57344, [NF, 512], BF16)

        def prefetch_wgv(l, extra_w=()):
            base = cnt["gv"]
            for i in range(NSLOT):
                slot = (base + i) % NSLOT
                P.op("sync", lambda e, slot=slot, i=i: e.dma_start(out=wgv_sb[:, slot, :], in_=wffn_b[l, i, :, 0:2048]),
                     r=[("wffn_b", l, ff) for ff in range(NF)], w=[("wgv", slot)] + list(extra_w), dma=("wgv", slot))
            return NSLOT

        def load_wd(l, tok=True):
            for hf in range(2):
                fs = slice(hf * 11, hf * 11 + 11)
                (OPA if tok else P.op)("sync", lambda e, fs=fs: e.dma_start(
                    out=wd_sb[:, fs, :], in_=wffn_b[l, fs, :, 2048:3072].rearrange("f p d -> p f d")),
                    r=[("wffn_b", l, f) for f in range(NF)], w=[("wd", hf)], dma=("wd", hf))

        def ffn(l, seq=0, last=False, wd_loaded=False, early=None, wgv_pre=0, pre0_done=False):
            load_gpost(l, 1)
            if not pre0_done:
                prenorm_many(l, [0, 1, 2, 3], 8, "nat")
            phase_enter()
            cw = 16
            cb = 16 + 66
            pending_io = []

            pending_ld = []

            def flush_st():
                for t in pending_io:
                    P.op("sync", lambda e, t=t: e.dma_start(out=y_d[seq, t * 128:(t + 1) * 128, :], in_=x_sb[:, t, :]),
                         r=[("x", t)], w=[("xout", seq, t)], dma=("xst", t))
                    pending_ld.append(t)
                del pending_io[:]

            def flush_ld():
                for t in pending_ld:
                    if seq + 1 < nseq:
                        P.op("sync", lambda e, t=t: e.dma_start(out=x_sb[:, t, :], in_=x_d[seq + 1, t * 128:(t + 1) * 128, :]),
                             w=[("x", t)], dma=("xld", t))
                del pending_ld[:]

            for tb in range(NTB):
                hkeys = [("ht", t) for t in range(tb * 4, tb * 4 + 4)]
                c0 = 16 + tb * 512
                nstat = {}
                nparts = {}
                for f in range(NF):
                    if tb + 1 < NTB and f in (0, 5, 10, 15):
                        tn = (tb + 1) * 4 + (0, 5, 10, 15).index(f)
                        nstat[tn] = prenorm_stats(tn)
                    if tb + 1 < NTB and f in (2, 7, 12, 17):
                        tn = (tb + 1) * 4 + (2, 7, 12, 17).index(f)
                        nparts[tn] = prenorm_parts(l, tn, 8, "nat", nstat[tn])
                        nparts[tn]["copy"]()
                    if tb + 1 < NTB and f in (4, 9, 14, 19):
                        tn = (tb + 1) * 4 + (4, 9, 14, 19).index(f)
                        nparts[tn]["pe"]()
                        nparts[tn]["evac"]()
                    if tb == 0 and f == 6 and not wd_loaded:
                        load_wd(l)
                    if f == 5 and pending_io:
                        flush_st()
                    if f == 14 and pending_ld:
                        flush_ld()
                    slot = cnt["gv"] % NSLOT
                    gs = cnt["gv"] % 2
                    cnt["gv"] += 1
                    if not (tb == 0 and f < wgv_pre):
                        OPA("sync", lambda e, slot=slot, f=f: e.dma_start(out=wgv_sb[:, slot, :], in_=wffn_b[l, f, :, 0:2048]),
                            r=[("wffn_b", l, ff) for ff in range(NF)], w=[("wgv", slot)], dma=("wgv", slot))
                    bg, bv = gs * 2, gs * 2 + 1
                    for (bank, woff) in ((bg, 0), (bv, 1024)):
                        for k in range(8):
                            OPA("tensor", lambda e, k=k, slot=slot, bank=bank, woff=woff, c0=c0: e.matmul(
                                ps[:, bank, :], wgv_sb[:, slot, woff + k * 128:woff + (k + 1) * 128], h_sb[:, k, c0:c0 + 512],
                                start=(k == 0), stop=(k == 7)),
                                r=[("wgv", slot)] + hkeys, w=[("ps", bank)])
                    ts = cnt["t"] % 2
                    cnt["t"] += 1
                    g_ps = ps[:, bg, :]
                    v_ps = ps[:, bv, :]
                    w0, w1, w2 = pp(l, cw + 0 * 22 + f), pp(l, cw + 1 * 22 + f), pp(l, cw + 2 * 22 + f)
                    bb = pp(l, cb + f)
                    tk = ("t0", ts)
                    P.op("scalar", lambda e, ts=ts, g_ps=g_ps, w2=w2, bb=bb: e.activation(
                        out=t0_sb[:, ts, :], in_=g_ps, func=AF.Identity, scale=w2, bias=bb),
                        r=[("ps", bg), "pp"], w=[tk])
                    if tb > 0:
                        P.op("vector", lambda e, ts=ts, f=f, w1=w1: e.scalar_tensor_tensor(
                            out=t0_sb[:, ts, 0:1], in0=halo_sb[:, f, 1:2], scalar=w1, in1=t0_sb[:, ts, 0:1],
                            op0=ALU.mult, op1=ALU.add), r=[("halo", f), tk, "pp"], w=[tk])
                    P.op("vector", lambda e, ts=ts, g_ps=g_ps, w1=w1: e.scalar_tensor_tensor(
                        out=t0_sb[:, ts, 1:512], in0=g_ps[:, 0:511], scalar=w1, in1=t0_sb[:, ts, 1:512],
                        op0=ALU.mult, op1=ALU.add), r=[("ps", bg), tk, "pp"], w=[tk])
                    P.op("vector", lambda e, ts=ts, g_ps=g_ps, w0=w0: e.scalar_tensor_tensor(
                        out=t0_sb[:, ts, 2:512], in0=g_ps[:, 0:510], scalar=w0, in1=t0_sb[:, ts, 2:512],
                        op0=ALU.mult, op1=ALU.add), r=[("ps", bg), tk, "pp"], w=[tk])
                    if tb > 0:
                        P.op("vector", lambda e, ts=ts, f=f, w0=w0: e.scalar_tensor_tensor(
                            out=t0_sb[:, ts, 0:2], in0=halo_sb[:, f, 0:2], scalar=w0, in1=t0_sb[:, ts, 0:2],
                            op0=ALU.mult, op1=ALU.add), r=[("halo", f), tk, "pp"], w=[tk])
                    if tb < NTB - 1:
                        P.op("vector", lambda e, f=f, g_ps=g_ps: e.tensor_copy(out=halo_sb[:, f, :], in_=g_ps[:, 510:512]),
                             r=[("ps", bg)], w=[("halo", f)])
                    P.op("scalar", lambda e, ts=ts: e.activation(out=t0_sb[:, ts, :], in_=t0_sb[:, ts, :],
                                                                  func=AF.Gelu_apprx_tanh), r=[tk], w=[tk])
                    OPA("vector", lambda e, ts=ts, f=f, v_ps=v_ps: e.tensor_tensor(
                        out=hdn_sb[:, f, :], in0=t0_sb[:, ts, :], in1=v_ps, op=ALU.mult),
                        r=[tk, ("ps", bv)], w=[("hdn", f)])
                for tt in range(4):
                    t = tb * 4 + tt
                    ds = cnt["dn"] % 2
                    cnt["dn"] += 1
                    b0 = 4 + ds * 2
                    for f in range(NF):
                        for dh in range(2):
                            OPA("tensor", lambda e, f=f, dh=dh, tt=tt, b0=b0: e.matmul(
                                ps[:, b0 + dh, :], hdn_sb[:, f, tt * 128:(tt + 1) * 128],
                                wd_sb[:, f, dh * 512:(dh + 1) * 512], start=(f == 0), stop=(f == NF - 1)),
                                r=[("hdn", f), ("wd", f // 11)], w=[("ps", b0 + dh)])
                    src = ps[:, b0:b0 + 2, :].rearrange("p a b -> p (a b)")
                    post_update(l, 1, src, [("ps", b0), ("ps", b0 + 1)], t)
                    if last:
                        pending_io.append(t)
                    if early is not None and t >= 1:
                        early[t - 1] = prenorm_stats(t - 1)
            if last:
                flush_st()
                flush_ld()
            if early is not None:
                early[NT - 1] = prenorm_stats(NT - 1)

        def pool_prep(j):
            phase_enter()
            T = arena[:, 0:4096].bitcast(F32)
            SC = arena[:, 4096:6144].bitcast(F32)
            STG = arena[:, 6144:8192]
            OPA("sync", lambda e: e.dma_start(out=T, in_=poolw_d[j]), w=["pp_T"], dma="pp_T")
            OPA("sync", lambda e: e.dma_start(out=SC, in_=pscale_d[j:j + 1, :].partition_broadcast(128)), w=["pp_S"], dma="pp_S")
            lpool = 2 * j + 1
            for kch in range(8):
                OPA("vector", lambda e, kch=kch: e.tensor_scalar(
                    out=T[:, kch * 256:(kch + 1) * 256], in0=T[:, kch * 256:(kch + 1) * 256],
                    scalar1=pp(lpool, 0, 8)[:, kch:kch + 1], scalar2=None, op0=ALU.mult), r=["pp_T", "pp"], w=["pp_T"])
            OPA("vector", lambda e: e.tensor_tensor(
                out=STG.rearrange("p (g k n) -> p g k n", g=4, k=2), in0=T.rearrange("p (g k n) -> p g k n", g=4, k=2),
                in1=mkap(SC, 0, [[256, 4], [0, 2], [1, 256]]), op=ALU.mult), r=["pp_T", "pp_S"], w=["pp_G"])
            OPA("sync", lambda e: e.dma_start(out=poolw_b[j], in_=STG), r=["pp_G"], w=[("poolw_b", j)], dma="pp_O")

        pre_n = [0]

        def pool_mixer(l, prefetch_wd=False, early=None):
            j = l // 2
            load_gpost(l, 0)
            phase_enter()
            if prefetch_wd:
                load_wd(l, tok=False)
                pre_n[0] = prefetch_wgv(l)
            wp = arena[:, 57344 // 2:(57344 + 4096) // 2].rearrange("p (g k n) -> p g k n", g=4, k=2)
            band = arena[:, 61440 // 2:(61440 + 3072) // 2].rearrange("p (m t) -> p m t", m=12)
            OPA("sync", lambda e: e.dma_start(out=wp, in_=poolw_b[j].rearrange("p (g k n) -> p g k n", g=4, k=2)),
                r=[("poolw_b", j)], w=["wp"], dma="wp")
            OPA("gpsimd", lambda e: e.dma_start(out=band, in_=band_d.rearrange("p (m t) -> p m t", m=12)), w=["band"], dma="band")
            early = early or {}
            gs_ = lambda t: early.pop(t) if t in early else prenorm_stats(t)
            stats = {0: gs_(0)}
            outs = {}
            for i in range(NT + 2):
                parts = None
                if i < NT:
                    parts = prenorm_parts(l, i, 0, "pool", stats[i], band=band, first=(i == 0))
                    parts["copy"]()
                pst = None
                if i - 2 >= 0:
                    src, keys = outs[i - 2]
                    pst = post_stats(src, keys)
                if parts is not None:
                    parts["pe"]()
                if i - 2 >= 0:
                    src, keys = outs[i - 2]
                    post_apply(0, src, keys, i - 2, pst)
                if i + 1 < NT:
                    stats[i + 1] = gs_(i + 1)
                if parts is not None:
                    parts["evac"]()
                if 0 <= i - 1 < NT:
                    outs[i - 1] = pool_mm(wp, i - 1)

        def pool_mm(wp, t):
            bsel = cnt["glu"] % 2
            cnt["glu"] += 1
            b0 = bsel * 2
            for g in range(4):
                for kk in range(2):
                    OPA("tensor", lambda e, g=g, kk=kk: e.matmul(
                        ps[:, b0 + g // 2, (g % 2) * 256:(g % 2) * 256 + 256],
                        h_sb[:, 2 * g + kk, 16 + t * 128:16 + (t + 1) * 128], wp[:, g, kk, :],
                        start=(kk == 0), stop=(kk == 1)),
                        r=[("ht", t), "wp"], w=[("ps", b0 + g // 2)])
            return ps[:, b0:b0 + 2, :].rearrange("p a b -> p (a b)"), [("ps", b0), ("ps", b0 + 1)]

        def s5_prep(j, after_loads=None):
            phase_enter()
            big = lambda i: arena[:, i * 2048:(i + 1) * 2048].bitcast(F32)
            Br, Bi, Cr, Ci, Cin, BBr, BBi, Mr, Mi, T1, T2, T3, T4 = [big(i) for i in range(13)]
            o = 13 * 4096
            Kst = arena[:, o // 2:(o + 2048) // 2].rearrange("p (a b) -> p a b", a=8)
            Wzst = arena[:, (o + 2048) // 2:(o + 6144) // 2].rearrange("p (a r b) -> p a r b", a=8, r=2)
            Wcst = arena[:, (o + 6144) // 2:(o + 10240) // 2].rearrange("p (r b) -> p r b", r=2)
            o2 = o + 10240
            sm = lambda i: arena[:, (o2 + i * 128) // 2:(o2 + (i + 1) * 128) // 2].bitcast(F32)
            par = arena[:, (o2 + 40 * 128) // 2:(o2 + 40 * 128 + 384) // 2].bitcast(F32)
            pw = arena[:, (o2 + 44 * 128) // 2:(o2 + 44 * 128 + 2304) // 2].bitcast(F32).rearrange("p (k r q) -> p k r q", k=9, r=2)
            lr, li, ld = par[:, 0:32], par[:, 32:64], par[:, 64:96]
            dt, lrdt, ang, mag, sn, cs, abr, abi, den, nr, fr, fi, ta, tb_, kf, ang2 = [sm(i) for i in range(16)]
            ki = sm(16).bitcast(I32)
            V = lambda fn, r, w: OPA("vector", fn, r, w, strict=True)
            A_ = lambda fn, r, w: OPA("scalar", fn, r, w, strict=True)
            K1 = ["s5sm"]
            OPA("sync", lambda e: e.dma_start(out=par, in_=s5p_d[j]), w=K1, dma="s5p")
            for i, tl in enumerate((Br, Bi, Cr, Ci)):
                OPA("sync", lambda e, i=i, tl=tl: e.dma_start(out=tl, in_=s5bc_d[j, i]), w=[("s5big", i)], dma=("s5bc", i))
            if after_loads is not None:
                after_loads()
            A_(lambda e: e.activation(out=dt, in_=ld, func=AF.Exp), K1, K1)
            V(lambda e: e.tensor_tensor(out=lrdt, in0=lr, in1=dt, op=ALU.mult), K1, K1)
            V(lambda e: e.tensor_tensor(out=ang, in0=li, in1=dt, op=ALU.mult), K1, K1)
            A_(lambda e: e.activation(out=mag, in_=lrdt, func=AF.Exp), K1, K1)
            C1, C2 = 6.28125, 0.0019353071795864769

            def sincos(dst, shift):
                V(lambda e: e.tensor_scalar(out=ang2, in0=ang, scalar1=shift, scalar2=None, op0=ALU.add), K1, K1)
                V(lambda e: e.tensor_scalar(out=ki, in0=ang2, scalar1=0.15915494309189535, scalar2=None, op0=ALU.mult), K1, K1)
                V(lambda e: e.tensor_copy(out=kf, in_=ki), K1, K1)
                V(lambda e: e.scalar_tensor_tensor(out=ta, in0=kf, scalar=-C1, in1=ang2, op0=ALU.mult, op1=ALU.add), K1, K1)
                V(lambda e: e.scalar_tensor_tensor(out=ta, in0=kf, scalar=-C2, in1=ta, op0=ALU.mult, op1=ALU.add), K1, K1)
                V(lambda e: e.tensor_scalar(out=ta, in0=ta, scalar1=-3.1415925, scalar2=3.1415925, op0=ALU.max, op1=ALU.min), K1, K1)
                A_(lambda e: e.activation(out=dst, in_=ta, func=AF.Sin), K1, K1)
            sincos(sn, 0.0)
            sincos(cs, 1.5707963267948966)
            V(lambda e: e.tensor_tensor(out=abr, in0=mag, in1=cs, op=ALU.mult), K1, K1)
            V(lambda e: e.tensor_tensor(out=abi, in0=mag, in1=sn, op=ALU.mult), K1, K1)
            V(lambda e: e.tensor_tensor(out=den, in0=lr, in1=lr, op=ALU.mult), K1, K1)
            V(lambda e: e.tensor_tensor(out=ta, in0=li, in1=li, op=ALU.mult), K1, K1)
            V(lambda e: e.tensor_tensor(out=den, in0=den, in1=ta, op=ALU.add), K1, K1)
            V(lambda e: e.reciprocal(out=den, in_=den), K1, K1)
            V(lambda e: e.tensor_scalar(out=nr, in0=abr, scalar1=-1.0, scalar2=None, op0=ALU.add), K1, K1)
            V(lambda e: e.tensor_tensor(out=ta, in0=nr, in1=lr, op=ALU.mult), K1, K1)
            V(lambda e: e.tensor_tensor(out=tb_, in0=abi, in1=li, op=ALU.mult), K1, K1)
            V(lambda e: e.tensor_tensor(out=ta, in0=ta, in1=tb_, op=ALU.add), K1, K1)
            V(lambda e: e.tensor_tensor(out=fr, in0=ta, in1=den, op=ALU.mult), K1, K1)
            V(lambda e: e.tensor_tensor(out=ta, in0=abi, in1=lr, op=ALU.mult), K1, K1)
            V(lambda e: e.tensor_tensor(out=tb_, in0=nr, in1=li, op=ALU.mult), K1, K1)
            V(lambda e: e.tensor_tensor(out=ta, in0=ta, in1=tb_, op=ALU.subtract), K1, K1)
            V(lambda e: e.tensor_tensor(out=fi, in0=ta, in1=den, op=ALU.mult), K1, K1)
            V(lambda e: e.memset(pw[:, 0, 0, :], 1.0), K1, K1)
            V(lambda e: e.memset(pw[:, 0, 1, :], 0.0), K1, K1)
            for k in range(8):
                V(lambda e, k=k: e.tensor_tensor(out=ta, in0=pw[:, k, 0, :], in1=abr, op=ALU.mult), K1, K1)
                V(lambda e, k=k: e.tensor_tensor(out=tb_, in0=pw[:, k, 1, :], in1=abi, op=ALU.mult), K1, K1)
                V(lambda e, k=k: e.tensor_tensor(out=pw[:, k + 1, 0, :], in0=ta, in1=tb_, op=ALU.subtract), K1, K1)
                V(lambda e, k=k: e.tensor_tensor(out=ta, in0=pw[:, k, 0, :], in1=abi, op=ALU.mult), K1, K1)
                V(lambda e, k=k: e.tensor_tensor(out=tb_, in0=pw[:, k, 1, :], in1=abr, op=ALU.mult), K1, K1)
                V(lambda e, k=k: e.tensor_tensor(out=pw[:, k + 1, 1, :], in0=ta, in1=tb_, op=ALU.add), K1, K1)
            V(lambda e: e.tensor_copy(out=scoef_sb[:, j, 0, 0:32], in_=pw[:, 8, 0, :]), K1, [("scoef", j)])
            V(lambda e: e.tensor_copy(out=scoef_sb[:, j, 0, 32:64], in_=pw[:, 8, 0, :]), K1, [("scoef", j)])
            V(lambda e: e.tensor_scalar(out=scoef_sb[:, j, 1, 0:32], in0=pw[:, 8, 1, :], scalar1=-1.0, scalar2=None, op0=ALU.mult),
              K1, [("scoef", j)])
            V(lambda e: e.tensor_copy(out=scoef_sb[:, j, 1, 32:64], in_=pw[:, 8, 1, :]), K1, [("scoef", j)])

            v3 = lambda ap: ap.rearrange("p (q c) -> p q c", q=32)
            bc = lambda ap32: ap32.unsqueeze(2).to_broadcast([128, 32, 32])

            def cmul(out_r, out_i, ar, ai, Xr, Xi, rk, wk):
                V(lambda e: e.tensor_tensor(out=v3(T1), in0=v3(Xr), in1=bc(ar), op=ALU.mult), rk, ["T1"])
                V(lambda e: e.tensor_tensor(out=v3(T2), in0=v3(Xi), in1=bc(ai), op=ALU.mult), rk, ["T2"])
                V(lambda e: e.tensor_tensor(out=out_r, in0=T1, in1=T2, op=ALU.subtract), ["T1", "T2"], [wk[0]])
                V(lambda e: e.tensor_tensor(out=v3(T1), in0=v3(Xi), in1=bc(ar), op=ALU.mult), rk, ["T1"])
                V(lambda e: e.tensor_tensor(out=v3(T2), in0=v3(Xr), in1=bc(ai), op=ALU.mult), rk, ["T2"])
                V(lambda e: e.tensor_tensor(out=out_i, in0=T1, in1=T2, op=ALU.add), ["T1", "T2"], [wk[1]])

            cmul(BBr, BBi, fr, fi, Br, Bi, K1 + [("s5big", 0), ("s5big", 1)], ["BBr", "BBi"])
            V(lambda e: e.tensor_scalar(out=Cin, in0=Ci, scalar1=-1.0, scalar2=None, op0=ALU.mult), [("s5big", 3)], ["Cin"])
            G_ = lambda fn, r, w: OPA("vector", fn, r, w)
            for k in range(8):
                s_z = 7 - k
                cmul(Mr, Mi, pw[:, k, 0, :], pw[:, k, 1, :], BBr, BBi, K1 + ["BBr", "BBi"], ["Mr", "Mi"])
                for jb in range(2):
                    pb = (cnt["za"] % 2) * 3
                    cnt["za"] += 1
                    for jl in range(4):
                        jt = jb * 4 + jl
                        sl = slice(jt * 128, (jt + 1) * 128)
                        csl = slice(jl * 128, (jl + 1) * 128)
                        OPA("tensor", lambda e, sl=sl, csl=csl, pb=pb: e.matmul(ps[:, pb, csl], Mr[:, sl], Cr[:, sl], start=True, stop=False),
                            r=["Mr", ("s5big", 2)], w=[("ps", pb)])
                        OPA("tensor", lambda e, sl=sl, csl=csl, pb=pb: e.matmul(ps[:, pb, csl], Mi[:, sl], Cin[:, sl], start=False, stop=True),
                            r=["Mi", "Cin"], w=[("ps", pb)])
                    for jl in range(4):
                        jt = jb * 4 + jl
                        sl = slice(jt * 128, (jt + 1) * 128)
                        csl = slice(jl * 128, (jl + 1) * 128)
                        OPA("tensor", lambda e, sl=sl, csl=csl, pb=pb: e.transpose(out=ps[:, pb + 1, csl], in_=Mr[:, sl], identity=ident_f[:]),
                            r=["Mr", "ident_f"], w=[("ps", pb + 1)])
                    for jl in range(4):
                        jt = jb * 4 + jl
                        sl = slice(jt * 128, (jt + 1) * 128)
                        csl = slice(jl * 128, (jl + 1) * 128)
                        OPA("tensor", lambda e, sl=sl, csl=csl, pb=pb: e.transpose(out=ps[:, pb + 2, csl], in_=Mi[:, sl], identity=ident_f[:]),
                            r=["Mi", "ident_f"], w=[("ps", pb + 2)])
                    js = slice(jb * 4, jb * 4 + 4)
                    V(lambda e, js=js, pb=pb: e.tensor_tensor(out=Kst[:, js, :], in0=ps[:, pb, :].rearrange("p (a b) -> p a b", a=4),
                                                             in1=mkap(mask_sb[:], 0, [[0, 4], [1, 128]]), op=ALU.mult),
                      [("ps", pb), "mask"], ["Kst"])
                    A_(lambda e, js=js, pb=pb: e.activation(out=Wzst[:, js, 0, :], in_=ps[:, pb + 1, :].rearrange("p (a b) -> p a b", a=4),
                                                            func=AF.Copy), [("ps", pb + 1)], ["Wzst"])
                    A_(lambda e, js=js, pb=pb: e.activation(out=Wzst[:, js, 1, :], in_=ps[:, pb + 2, :].rearrange("p (a b) -> p a b", a=4),
                                                            func=AF.Copy), [("ps", pb + 2)], ["Wzst"])
                OPA("sync", lambda e, k=k: e.dma_start(out=s5w_b[j, :, :, 2048 + k * 128:2048 + (k + 1) * 128].rearrange("t p c -> p t c"), in_=Kst),
                    r=["Kst"], w=[("s5w_b", j, "k", k)], dma="s5k")
                OPA("sync", lambda e, s_z=s_z: e.dma_start(
                    out=s5w_b[j, :, :, s_z * 256:(s_z + 1) * 256].rearrange("t p (r c) -> p t r c", r=2), in_=Wzst),
                    r=["Wzst"], w=[("s5w_b", j, "z", k)], dma="s5z")
                G_(lambda e, k=k: e.tensor_tensor(out=v3(T3), in0=v3(Cr), in1=bc(pw[:, k + 1, 0, :]), op=ALU.mult), K1 + [("s5big", 2)], ["T3"])
                G_(lambda e, k=k: e.tensor_tensor(out=v3(T4), in0=v3(Ci), in1=bc(pw[:, k + 1, 1, :]), op=ALU.mult), K1 + [("s5big", 3)], ["T4"])
                G_(lambda e: e.tensor_tensor(out=Wcst[:, 0, :], in0=T3, in1=T4, op=ALU.subtract), ["T3", "T4"], ["Wcst"])
                G_(lambda e, k=k: e.tensor_tensor(out=v3(T3), in0=v3(Cr), in1=bc(pw[:, k + 1, 1, :]), op=ALU.mult), K1 + [("s5big", 2)], ["T3"])
                G_(lambda e, k=k: e.tensor_tensor(out=v3(T4), in0=v3(Ci), in1=bc(pw[:, k + 1, 0, :]), op=ALU.mult), K1 + [("s5big", 3)], ["T4"])
                G_(lambda e: e.tensor_tensor(out=T3, in0=T3, in1=T4, op=ALU.add), ["T3", "T4"], ["T3"])
                G_(lambda e: e.tensor_scalar(out=Wcst[:, 1, :], in0=T3, scalar1=-1.0, scalar2=None, op0=ALU.mult), ["T3"], ["Wcst"])
                for ri in range(2):
                    OPA("sync", lambda e, k=k, ri=ri: e.dma_start(
                        out=s5w_b[j, :, :, 3072 + k * 256 + ri * 128:3072 + k * 256 + (ri + 1) * 128].rearrange("t p c -> p t c"),
                        in_=Wcst[:, ri, :].rearrange("p (t c) -> p t c", t=8)),
                        r=["Wcst"], w=[("s5w_b", j, "c", k) if ri == 0 else ("s5w_b", j, "c2", k)], dma="s5c")

        S5W_KEYS = lambda j: [("s5w_b", j, a, k) for a in ("k", "z", "c", "c2") for k in range(8)]

        def s5_mixer(l, early=None, next_ffn=False, ffn_pre=None):
            j = l // 2
            load_gpost(l, 0)
            prenorm_many(l, list(range(NT)), 0, "deint", early=early)
            phase_enter()
            ZS = carve(0, [64, 256], BF16)
            ring = arena[:, 32768 // 2:(32768 + 8192) // 2].bitcast(F32).rearrange("p (s e) -> p s e", s=32)
            y32 = arena[:, 32768 // 2:(32768 + 8192) // 2].bitcast(F32)
            sg32 = arena[:, 32768 // 2:(32768 + 4096) // 2].bitcast(F32).rearrange("p (a b) -> p a b", a=2)
            bund = arena[:, 40960 // 2:(40960 + 20480) // 2].rearrange("p (a b) -> p a b", a=2)
            wgl = arena[:, 61440 // 2:(61440 + 16384) // 2].rearrange("p (k d) -> p k d", k=8)
            tA = arena[:, 77824 // 2:(77824 + 256) // 2].bitcast(F32)
            tB = arena[:, 78080 // 2:(78080 + 256) // 2].bitcast(F32)
            mfeat = arena[:, 0:16384].bitcast(F32).rearrange("p (a m b) -> p a m b", a=2, m=8)
            RK = [("ring", 0), ("ring", 1)]
            ZK = [("zs", q) for q in range(32)]
            dsk = pp_sb[:, PP_S5 + j * 16: PP_S5 + j * 16 + 8]
            bgl = pp_sb[:, PP_S5 + j * 16 + 8: PP_S5 + j * 16 + 16]
            def load_wgl():
                OPA("sync", lambda e: e.dma_start(out=wgl, in_=wglu_b[j].rearrange("p (k d) -> p k d", k=8)),
                    r=[("wglu_b", j)], w=["wgl"], dma="wgl")

            def load_bund(jt, part):
                slot = cnt["bund"] % 2
                cnt["bund"] += 1
                c0_, c1_ = (0, 2048) if part == "A" else (2048, 5120)
                OPA("sync", lambda e: e.dma_start(out=bund[:, slot, c0_:c1_], in_=s5w_b[j, jt, :, c0_:c1_]),
                    r=S5W_KEYS(j), w=[("bund", slot)], dma=("bund", slot))
                return slot

            for jt in range(8):
                slot = load_bund(jt, "A")
                pb = (jt % 2) * 4
                for ri in range(2):
                    for s in range(8):
                        wo = s * 256 + ri * 128
                        for ql in range(4):
                            OPA("tensor", lambda e, ql=ql, ri=ri, s=s, wo=wo, pb=pb, jt=jt, slot=slot: e.matmul(
                                ps[:, pb + ql, ri * 256:(ri + 1) * 256], bund[32 * ql:32 * ql + 32, slot, wo:wo + 128],
                                h_sb[32 * ql:32 * ql + 32, jt, 16 + s * 256:16 + (s + 1) * 256],
                                start=(s == 0), stop=(s == 7), tile_position=(32 * ql, 0)),
                                r=[("bund", slot), ("hk", jt)], w=[("ps", pb + ql)])
                for ql in range(4):
                    q = 4 * jt + ql
                    src = ps[:, pb + ql, :].rearrange("p (r c) -> p r c", r=2)
                    dst = mkap(ZS, q * 256, [[32 * 256, 2], [1, 256]])
                    if q % 2 == 0:
                        OPA("scalar", lambda e, src=src, dst=dst: e.activation(out=dst, in_=src, func=AF.Copy),
                            r=[("ps", pb + ql)], w=[("zs", q)])
                    else:
                        OPA("vector", lambda e, src=src, dst=dst: e.tensor_copy(out=dst, in_=src),
                            r=[("ps", pb + ql)], w=[("zs", q)])

            if DEBUG:
                OPA("sync", lambda e: e.dma_start(out=dbg_w, in_=s5w_b[j]), r=S5W_KEYS(j), w=["dbgout"], dma="dbg")
                OPA("sync", lambda e: e.dma_start(out=dbg_sc, in_=scoef_sb[:].rearrange("p a b c -> p (a b c)")), r=[("scoef", j)], w=["dbgout"], dma="dbg")
                OPA("sync", lambda e: e.dma_start(out=dbg_z, in_=arena[:, 0:16384]), r=ZK, w=["dbgout", "dbgz"], dma="dbg")
                OPA("sync", lambda e: e.dma_start(out=dbg_u, in_=h_sb[:].rearrange("p a b -> p (a b)")), r=HK, w=["dbgout"], dma="dbg")
                ZK = ZK + ["dbgz"]
            A1 = scoef_sb[:, j, 0, :]
            A2 = scoef_sb[:, j, 1, :]
            SE = scan_eng
            zcol = lambda c: mkap(ZS, c, [[256, 64]])
            OPA(SE, lambda e: e.tensor_copy(out=ring[:, 0, :], in_=zcol(0)), r=ZK + RK, w=[("ring", 0)])
            for c in range(1, 256):
                cur = ring[:, (c - 1) % 32, :]
                nxt = ring[:, c % 32, :]
                swp = mkap(cur, 32, [[-32, 2], [1, 32]])
                hk = ("ring", (c % 32) // 16)
                OPA(SE, lambda e, cur=cur: e.tensor_tensor(out=tA, in0=cur, in1=A1, op=ALU.mult), r=[("scoef", j)], w=["tA"])
                OPA(SE, lambda e, swp=swp: e.tensor_tensor(out=tB.rearrange("p (a b) -> p a b", a=2), in0=swp,
                                                           in1=A2.rearrange("p (a b) -> p a b", a=2), op=ALU.mult),
                    r=[("scoef", j)], w=["tB"])
                OPA(SE, lambda e: e.tensor_tensor(out=tA, in0=tA, in1=tB, op=ALU.add), r=["tB"], w=["tA"])
                OPA(SE, lambda e, nxt=nxt, c=c: e.tensor_tensor(out=nxt, in0=tA, in1=zcol(c), op=ALU.add), r=["tA"], w=[hk])
                if c % 16 == 15:
                    b = c // 16
                    half = b % 2
                    src = ring[:, half * 16:half * 16 + 16, :].rearrange("p s e -> p e s")
                    dst = ZS[:, :, b * 16:b * 16 + 16]
                    OPA("scalar", lambda e, src=src, dst=dst: e.activation(out=dst, in_=src, func=AF.Copy),
                        r=[("ring", half)], w=["zs_conv"])

            if DEBUG:
                OPA("sync", lambda e: e.dma_start(out=dbg_s, in_=arena[:, 0:16384]), r=["zs_conv"], w=["dbgout"], dma="dbg")
            for jt in range(8):
                slot = load_bund(jt, "C")
                if jt == 1:
                    load_wgl()
                yb = (cnt["y"] % 2) * 4
                cnt["y"] += 1
                Y = ps[:, yb:yb + 4, :].rearrange("p b (h c) -> p (b h) c", h=2)
                ykeys = [("ps", yb + i) for i in range(4)]
                for s in range(8):
                    for k in range(s + 1):
                        OPA("tensor", lambda e, s=s, k=k, jt=jt, Y=Y, slot=slot: e.matmul(
                            Y[:, s, :], bund[:, slot, 2048 + k * 128:2048 + (k + 1) * 128],
                            h_sb[:, jt, 16 + (s - k) * 256:16 + (s - k + 1) * 256], start=(k == 0), stop=False),
                            r=[("bund", slot), ("hk", jt)], w=[("ps", yb + s // 2)])
                    for ri in range(2):
                        for ql in range(4):
                            q = 4 * jt + ql
                            wo = 3072 + s * 256 + ri * 128 + ql * 32
                            last = (ql == 3 and ri == 1)
                            OPA("tensor", lambda e, s=s, ql=ql, ri=ri, q=q, wo=wo, last=last, Y=Y, slot=slot: e.matmul(
                                Y[32 * ql:32 * ql + 32, s, 1:256], bund[:, slot, wo:wo + 32], ZS[:, ri * 32 + q, 0:255],
                                start=False, stop=last, tile_position=(0, 32 * ql), skip_group_check=True),
                                r=[("bund", slot), "zs_conv"], w=[("ps", yb + s // 2)])
                y32v = mkap(y32, 0, [[1, 8], [8, 256]])
                u_de = h_sb[:, jt, 16:HW].rearrange("p (s c) -> p s c", s=8)
                OPA("vector", lambda e, jt=jt, Y=Y, y32v=y32v, u_de=u_de: e.scalar_tensor_tensor(
                    out=y32v, in0=u_de, scalar=dsk[:, jt:jt + 1], in1=Y, op0=ALU.mult, op1=ALU.add),
                    r=ykeys + [("hk", jt), "pp"], w=RK + ["y32"])
                OPA("scalar", lambda e, jt=jt: e.activation(out=h_sb[:, jt, 16:HW], in_=y32, func=AF.Gelu_apprx_tanh),
                    r=["y32"] + RK, w=[("yg", jt), ("hk", jt)])

            if DEBUG:
                OPA("sync", lambda e: e.dma_start(out=dbg_h, in_=h_sb[:].rearrange("p a b -> p (a b)")), r=[("yg", k) for k in range(8)], w=["dbgout"], dma="dbg")
            YG = [("yg", k) for k in range(8)]

            def glu_mm_one(tb, mslot, m):
                c0 = 16 + tb * 512
                hkb = [("ht", t) for t in range(tb * 4, tb * 4 + 4)]
                bank = cnt["glu"] % 4
                cnt["glu"] += 1
                for k in range(8):
                    OPA("tensor", lambda e, k=k: e.matmul(
                        ps[:, bank, :], wgl[:, k, m * 128:(m + 1) * 128], h_sb[:, k, c0:c0 + 512],
                        start=(k == 0), stop=(k == 7)), r=["wgl"] + YG + hkb, w=[("ps", bank)])
                ss_ = cnt["sg"] % 2
                cnt["sg"] += 1
                OPA("scalar", lambda e: e.activation(
                    out=sg32[:, ss_, :], in_=ps[:, bank, :], func=AF.Sigmoid, bias=bgl[:, m:m + 1]),
                    r=[("ps", bank), "pp"], w=[("sg", ss_)] + RK)
                OPA("vector", lambda e: e.tensor_tensor(
                    out=mfeat[:, mslot, m, :], in0=h_sb[:, m, c0:c0 + 512], in1=sg32[:, ss_, :], op=ALU.mult),
                    r=[("sg", ss_), ("yg", m)] + hkb, w=[("mf", mslot), "zs_conv"])

            def glu_out_tile(tb, mslot, tt):
                t = tb * 4 + tt
                ds = cnt["dn"] % 2
                cnt["dn"] += 1
                b0 = 4 + ds * 2
                for m in range(8):
                    OPA("tensor", lambda e, m=m: e.transpose(
                        out=ps[:, b0 + m // 4, (m % 4) * 128:(m % 4) * 128 + 128],
                        in_=mfeat[:, mslot, m, tt * 128:(tt + 1) * 128], identity=ident_f[:]),
                        r=[("mf", mslot), "ident_f"], w=[("ps", b0 + m // 4)])
                src = ps[:, b0:b0 + 2, :].rearrange("p a b -> p (a b)")
                post_update(l, 0, src, [("ps", b0), ("ps", b0 + 1)], t)

            if next_ffn:
                pre_n[0] = prefetch_wgv(l, extra_w=[("bund", 0), ("bund", 1)])
            for tb in range(NTB):
                for m in range(8):
                    glu_mm_one(tb, tb % 2, m)
                    if tb > 0 and m % 2 == 1:
                        glu_out_tile(tb - 1, (tb - 1) % 2, m // 2)
                    if tb == 1 and m == 7 and ffn_pre is not None:
                        ffn_pre()
            for tt in range(4):
                glu_out_tile(NTB - 1, (NTB - 1) % 2, tt)

        s5_js = sorted({l // 2 for l in layers if l % 2 == 0}) if do_mixer else []
        pool_js = sorted({l // 2 for l in layers if l % 2 == 1}) if do_mixer else []
        def first_x_loads():
            for t in range(NT):
                P.op("sync", lambda e, t=t: e.dma_start(out=x_sb[:, t, :], in_=x_d[0, t * 128:(t + 1) * 128, :]),
                     w=[("x", t)], dma=("xld", t))

        def bulk_casts():
            gate = [("s5w_b", s5_js[-1], "c2", 7)] if s5_js else []
            for j in s5_js:
                P.op("gpsimd", lambda e, j=j: e.dma_start(out=wglu_b[j], in_=wglu_d[j]), r=gate, w=[("wglu_b", j)], dma=("castg", j))
            if do_ffn:
                cast_layer(layers[0])

        if s5_js:
            for i, j in enumerate(s5_js):
                s5_prep(j, after_loads=first_x_loads if i == 0 else None)
        else:
            first_x_loads()
        for j in pool_js:
            pool_prep(j)
        bulk_casts()
        for s in range(nseq):
            if s > 0 and not do_ffn:
                for t in range(NT):
                    P.op("sync", lambda e, s=s, t=t: e.dma_start(out=x_sb[:, t, :], in_=x_d[s, t * 128:(t + 1) * 128, :]),
                         w=[("x", t)], dma=("xld", t))
            early = None
            for li, l in enumerate(layers):
                if s == 0 and li + 1 < nl and do_ffn:
                    cast_layer(layers[li + 1])
                pre = False
                p0 = False
                pre_n[0] = 0
                if do_mixer:
                    if l % 2 == 0:
                        s5_mixer(l, early=early, next_ffn=do_ffn)
                    else:
                        pre = do_ffn
                        pool_mixer(l, prefetch_wd=pre, early=early)
                early = {} if (do_mixer and do_ffn and li + 1 < nl) else None
                if do_ffn:
                    ffn(l, seq=s, last=(li == nl - 1), wd_loaded=pre, early=early, wgv_pre=pre_n[0], pre0_done=p0)
            if not do_ffn:
                for t in range(NT):
                    P.op("sync", lambda e, s=s, t=t: e.dma_start(out=y_d[s, t * 128:(t + 1) * 128, :], in_=x_sb[:, t, :]),
                         r=[("x", t)], w=[("xout", s, t)], dma=("xst", t))
        P.op("sync", None, r=[("xout", s, t) for s in range(nseq) for t in range(NT)] + (["dbgout"] if DEBUG else []))
        P.emit(nc, st)
    return nc


def host_layout(inp):
    f32 = np.float32
    g = lambda k: np.asarray(inp[k], f32)
    wg = g("ffn_w_gate").reshape(DEPTH, 8, 128, NF, 128).transpose(0, 3, 2, 1, 4).reshape(DEPTH, NF, 128, 1024)
    wv = g("ffn_w_val").reshape(DEPTH, 8, 128, NF, 128).transpose(0, 3, 2, 1, 4).reshape(DEPTH, NF, 128, 1024)
    wd = g("ffn_w_down").reshape(DEPTH, NF, 128, 1024)
    wffn = np.ascontiguousarray(np.concatenate([wg, wv, wd], axis=3))
    cols = []
    for l in range(DEPTH):
        cols.append(g("norm_mix_pre")[l].reshape(8, 128).T)
        cols.append(g("norm_ffn_pre")[l].reshape(8, 128).T)
        cw = g("ffn_conv_w")[l].reshape(3, NF, 128)
        cols.append(cw.transpose(2, 0, 1).reshape(128, 66))
        cols.append(g("ffn_conv_b")[l].reshape(NF, 128).T)
    for j in range(2):
        cols.append(g("s5_d")[j].reshape(8, 128).T)
        cols.append(g("s5_b_glu")[j].reshape(8, 128).T)
    pp = np.ascontiguousarray(np.concatenate(cols, axis=1))
    gpost = np.ascontiguousarray(np.stack([g("norm_mix_post"), g("norm_ffn_post")], axis=1))
    ident = np.eye(128, dtype=f32)
    mask = np.kron(np.eye(8, dtype=f32), np.ones((16, 16), f32))
    perm = np.zeros((128, 128), f32)
    for i in range(128):
        perm[i, (i % 8) * 16 + i // 8] = 1.0
    band = np.zeros((12, 128, 128), f32)
    for gi, win in enumerate((2, 4, 8, 16)):
        for t in range(128):
            for jj in range(win):
                tp = t - jj
                if tp >= 0:
                    band[gi, tp, t] += 1.0 / win
                    band[8 + gi, tp, t] += 1.0 / min(t + 1, win)
                else:
                    band[4 + gi, 128 + tp, t] += 1.0 / win
            band[gi, t, t] -= 1.0
            band[8 + gi, t, t] -= 1.0
    band = np.ascontiguousarray(band.transpose(1, 0, 2).reshape(128, 12 * 128))
    poolw = np.ascontiguousarray(g("pool_w").reshape(2, 4, 2, 128, 256).transpose(0, 3, 1, 2, 4).reshape(2, 128, 2048))
    pscale = g("pool_scale")
    def gp(a):
        return a.reshape(2, 32, 2, 64).transpose(0, 2, 3, 1).reshape(2, 128, 32)
    ld = np.broadcast_to(g("s5_log_dt").reshape(2, 32, 2).transpose(0, 2, 1)[:, :, None, :], (2, 2, 64, 32)).reshape(2, 128, 32)
    s5p = np.ascontiguousarray(np.concatenate([gp(g("s5_lambda_re")), gp(g("s5_lambda_im")), ld], axis=2))
    def padB(b):
        out = np.zeros((2, 2, 64, 32, 2, 16), f32)
        bb = b.reshape(2, 32, 2, 64, 16)
        for par in range(2):
            out[:, par, :, :, par, :] = bb[:, :, par].transpose(0, 2, 1, 3)
        return out.reshape(2, 128, 1024)
    def padC(c):
        return padB(c.transpose(0, 1, 3, 2))
    s5bc = np.ascontiguousarray(np.stack([padB(g("s5_b_re")), padB(g("s5_b_im")), padC(g("s5_c_re")), padC(g("s5_c_im"))], axis=1))
    wglu = np.ascontiguousarray(g("s5_w_glu").reshape(2, 8, 128, 1024).transpose(0, 2, 1, 3).reshape(2, 128, 8192))
    return dict(wffn=wffn, pp=pp, gpost=gpost, ident=ident, mask=mask, perm=perm, band=band, poolw=poolw, pscale=pscale,
                s5p=s5p, s5bc=s5bc, wglu=wglu)


def kernel(**inputs):
    x = np.asarray(inputs["x"], np.float32)
    shared = host_layout(inputs)
    nseq = x.shape[0] // NCORES
    nc = build(nseq=nseq)
    in_maps = [dict(shared, x=np.ascontiguousarray(x[c * nseq:(c + 1) * nseq])) for c in range(NCORES)]
    res = run_bass_kernel_spmd(nc, in_maps, core_ids=list(range(NCORES)))
    return np.concatenate([r["y"] for r in res.results], axis=0)
```

```python
import contextlib
import numpy as np
import concourse.bass as bass
import concourse.mybir as mybir
from concourse.bass_utils import run_bass_kernel_spmd

F32 = mybir.dt.float32
BF16 = mybir.dt.bfloat16
I32 = mybir.dt.int32
AF = mybir.ActivationFunctionType
ALU = mybir.AluOpType

D = 1024
L_SEQ = 2048
DFF = 2816
NF = DFF // 128
NT = L_SEQ // 128
NTB = 4
DEPTH = 4
NCORES = 8
EPS = 1e-6
ENGS = ["sync", "gpsimd", "scalar", "vector", "tensor"]
DEBUG = False


class Prog:
    def __init__(self):
        self.ops = []
        self.lastw = {}
        self.readers = {}
        self.dma_cnt = {}

    def op(self, eng, fn, r=(), w=(), dma=None, strict=False):
        i = len(self.ops)
        deps = set()
        for k in r:
            if k in self.lastw:
                deps.add(self.lastw[k])
        for k in w:
            if k in self.lastw:
                deps.add(self.lastw[k])
            deps.update(self.readers.get(k, ()))
        for k in w:
            self.lastw[k] = i
            self.readers[k] = []
        for k in r:
            self.readers.setdefault(k, []).append(i)
        dval = None
        if dma is not None:
            self.dma_cnt[dma] = self.dma_cnt.get(dma, 0) + 16
            dval = self.dma_cnt[dma]
        deps.discard(i)
        self.ops.append(dict(eng=eng, fn=fn, deps=deps, dma=dma, dval=dval, inc=False, tick=None, strict=strict))
        return i

    def emit(self, nc, stack):
        ops = self.ops
        for o in ops:
            best = {}
            keep = []
            for d in o["deps"]:
                p = ops[d]
                if p["dma"] is not None:
                    keep.append(d)
                elif p["eng"] == o["eng"] and o["dma"] is None and not o["strict"]:
                    continue
                elif p["eng"] == o["eng"] and o["dma"] is not None and not o["strict"]:
                    continue
                else:
                    if p["eng"] not in best or best[p["eng"]] < d:
                        best[p["eng"]] = d
            dmax = {}
            for d in keep:
                kk = ops[d]["dma"]
                if kk not in dmax or ops[dmax[kk]]["dval"] < ops[d]["dval"]:
                    dmax[kk] = d
            keep = list(dmax.values())
            keep.extend(best.values())
            o["deps"] = sorted(keep)
            for d in best.values():
                ops[d]["inc"] = True
        cnt = {e: 0 for e in ENGS}
        for o in ops:
            if o["dma"] is None and o["inc"]:
                cnt[o["eng"]] += 1
                o["tick"] = cnt[o["eng"]]
        sems = {e: stack.enter_context(nc.semaphore("s_" + e)) for e in ENGS}
        dsems = {}
        for k in self.dma_cnt:
            dsems[k] = stack.enter_context(nc.semaphore("d_" + "_".join(str(x) for x in (k if isinstance(k, tuple) else (k,)))))
        block = stack.enter_context(nc.Block())

        def run(eng_name):
            def body(e):
                waited = {}
                for o in ops:
                    if o["eng"] != eng_name:
                        continue
                    for d in o["deps"]:
                        p = ops[d]
                        if p["dma"] is not None:
                            key, val = ("d", p["dma"]), p["dval"]
                            sem = dsems[p["dma"]]
                        else:
                            key, val = ("e", p["eng"]), p["tick"]
                            sem = sems[p["eng"]]
                        if waited.get(key, 0) >= val:
                            continue
                        waited[key] = val
                        e.wait_ge(sem, val)
                    if o["fn"] is None:
                        continue
                    ins = o["fn"](e)
                    if o["dma"] is not None:
                        ins.then_inc(dsems[o["dma"]], 16)
                    elif o["inc"]:
                        ins.then_inc(sems[eng_name], 1)
            return body

        block.sync(run("sync"))
        block.gpsimd(run("gpsimd"))
        block.scalar(run("scalar"))
        block.vector(run("vector"))
        block.tensor(run("tensor"))


def build(nseq=4, layers=(0, 1, 2, 3), do_mixer=True, do_ffn=True, scan_eng="vector", pool_eng="gpsimd"):
    nc = bass.Bass("TRN2", target_bir_lowering=False)
    nl = len(layers)
    x_d = nc.dram_tensor("x", [nseq, L_SEQ, D], F32, kind="ExternalInput").ap()
    y_d = nc.dram_tensor("y", [nseq, L_SEQ, D], F32, kind="ExternalOutput").ap()
    wffn_d = nc.dram_tensor("wffn", [DEPTH, NF, 128, 3072], F32, kind="ExternalInput").ap()
    wffn_b = nc.dram_tensor("wffn_b", [DEPTH, NF, 128, 3072], BF16, kind="Internal").ap()
    PP_L = 8 + 8 + 66 + 22
    PP_S5 = DEPTH * PP_L
    PP_W = PP_S5 + 32
    pp_d = nc.dram_tensor("pp", [128, PP_W], F32, kind="ExternalInput").ap()
    gpost_d = nc.dram_tensor("gpost", [DEPTH, 2, D], F32, kind="ExternalInput").ap()
    ident_d = nc.dram_tensor("ident", [128, 128], F32, kind="ExternalInput").ap()
    mask_d = nc.dram_tensor("mask", [128, 128], F32, kind="ExternalInput").ap()
    perm_d = nc.dram_tensor("perm", [128, 128], F32, kind="ExternalInput").ap()
    band_d = nc.dram_tensor("band", [128, 12 * 128], F32, kind="ExternalInput").ap()
    poolw_d = nc.dram_tensor("poolw", [2, 128, 2048], F32, kind="ExternalInput").ap()
    pscale_d = nc.dram_tensor("pscale", [2, D], F32, kind="ExternalInput").ap()
    poolw_b = nc.dram_tensor("poolw_b", [2, 128, 2048], BF16, kind="Internal").ap()
    s5p_d = nc.dram_tensor("s5p", [2, 128, 96], F32, kind="ExternalInput").ap()
    s5bc_d = nc.dram_tensor("s5bc", [2, 4, 128, 1024], F32, kind="ExternalInput").ap()
    wglu_d = nc.dram_tensor("wglu", [2, 128, 8192], F32, kind="ExternalInput").ap()
    wglu_b = nc.dram_tensor("wglu_b", [2, 128, 8192], BF16, kind="Internal").ap()
    s5w_b = nc.dram_tensor("s5w_b", [2, 8, 128, 5120], BF16, kind="Internal").ap()

    if DEBUG:
        dbg_d = nc.dram_tensor("dbg", [128, 8, 2048], BF16, kind="ExternalOutput").ap()
        dbg_w = nc.dram_tensor("dbg_w", [8, 128, 5120], BF16, kind="ExternalOutput").ap()
        dbg_sc = nc.dram_tensor("dbg_sc", [128, 256], F32, kind="ExternalOutput").ap()
        dbg_z = nc.dram_tensor("dbg_z", [128, 64 * 256], BF16, kind="ExternalOutput").ap()
        dbg_s = nc.dram_tensor("dbg_s", [128, 64 * 256], BF16, kind="ExternalOutput").ap()
        dbg_h = nc.dram_tensor("dbg_h", [128, 8 * (L_SEQ + 16)], BF16, kind="ExternalOutput").ap()
        dbg_u = nc.dram_tensor("dbg_u", [128, 8 * (L_SEQ + 16)], BF16, kind="ExternalOutput").ap()
    P = Prog()
    st = contextlib.ExitStack()
    with st:
        def sb(name, shape, dt):
            return st.enter_context(nc.sbuf_tensor(name, shape, dt))

        HW = L_SEQ + 16
        x_sb = sb("x_sb", [128, NT, D], F32)
        h_sb = sb("h_sb", [128, 8, HW], BF16)
        ARENA = 80 * 1024
        arena = sb("arena", [128, ARENA // 2], BF16)

        def carve(off, shape, dt):
            n = int(np.prod(shape))
            nb = n * (2 if dt == BF16 else 4)
            a = arena[:, off // 2:(off + nb) // 2]
            if dt != BF16:
                a = a.bitcast(dt)
            if len(shape) == 2:
                a = a.rearrange("p (a b) -> p a b", a=shape[0])
            elif len(shape) == 3:
                a = a.rearrange("p (a b c) -> p a b c", a=shape[0], b=shape[1])
            return a

        def mkap(base, add_off, dims):
            return bass.AP(base.tensor, base.offset + add_off, [list(base.ap[0])] + [list(d) for d in dims])

        halo_sb = sb("halo_sb", [128, NF, 2], F32)
        t0_sb = sb("t0_sb", [128, 2, 512], F32)
        junk_sb = sb("junk_sb", [128, D], BF16)
        hb_sb = sb("hb_sb", [128, 3, D], BF16)
        tmp_sb = sb("tmp_sb", [128, 1, D], F32)
        ss_sb = sb("ss_sb", [128, 64], F32)
        rs_sb = sb("rs_sb", [128, 64], F32)
        pp_sb = sb("pp_sb", [128, PP_W], F32)
        gpost_sb = sb("gpost_sb", [128, 2, D], F32)
        cst_sb = sb("cst_sb", [128, 4], F32)
        ident_f = sb("ident_f", [128, 128], F32)
        ident_b = sb("ident_b", [128, 128], BF16)
        mask_sb = sb("mask_sb", [128, 128], F32)
        perm_b = sb("perm_b", [128, 128], BF16)
        scoef_sb = sb("scoef_sb", [128, 2, 2, 64], F32)
        ps = st.enter_context(nc.psum_tensor("ps", [128, 8, 512], F32))

        def OPA(eng, fn, r=(), w=(), dma=None, strict=False):
            return P.op(eng, fn, list(r) + ["arena"], w, dma, strict)

        def phase_enter():
            P.op("vector", lambda e: e.memset(cst_sb[:, 1:2], 0.0), w=["arena"])

        P.op("sync", lambda e: e.dma_start(out=pp_sb[:], in_=pp_d), w=["pp"], dma="c_pp")
        P.op("sync", lambda e: e.dma_start(out=ident_f[:], in_=ident_d), w=["ident_f"], dma="c_id")
        P.op("sync", lambda e: e.dma_start(out=mask_sb[:], in_=mask_d), w=["mask"], dma="c_mk")
        P.op("gpsimd", lambda e: e.dma_start(out=perm_b[:], in_=perm_d), w=["perm_b"], dma="c_pm")
        P.op("vector", lambda e: e.tensor_copy(out=ident_b[:], in_=ident_f[:]), r=["ident_f"], w=["ident_b"])
        P.op("vector", lambda e: e.memset(cst_sb[:, 0:1], -0.5), w=["cst"])
        P.op("vector", lambda e: e.memset(h_sb[:, :, 0:16], 0.0), w=["hpad"])

        def cast_layer(l):
            for f in range(NF):
                P.op("gpsimd", lambda e, l=l, f=f: e.dma_start(out=wffn_b[l, f], in_=wffn_d[l, f]),
                     w=[("wffn_b", l, f)], dma=("cast", l))

        def pp(l, off, n=1):
            return pp_sb[:, l * PP_L + off: l * PP_L + off + n]

        cnt = {"ss": 0, "gv": 0, "t": 0, "hb": 0, "dn": 0, "bund": 0, "za": 0, "y": 0, "glu": 0, "sg": 0, "mf": 0, "pk": 0}
        HK = [("hk", k) for k in range(8)]
        HT = [("ht", t) for t in range(NT)]

        def rms_stats(src_ap, key_r, width=D):
            c = cnt["ss"] % 64
            cnt["ss"] += 1
            P.op("scalar", lambda e: e.activation(out=junk_sb[:, 0:width], in_=src_ap, func=AF.Square,
                                                   accum_out=ss_sb[:, c:c + 1]),
                 r=key_r, w=["junk", ("ss", c)])
            P.op("vector", lambda e: e.tensor_scalar(out=ss_sb[:, c:c + 1], in0=ss_sb[:, c:c + 1], scalar1=1.0 / width,
                                                      scalar2=EPS, op0=ALU.mult, op1=ALU.add),
                 r=[("ss", c)], w=[("ss", c)])
            P.op("gpsimd", lambda e: e.tensor_tensor(out=rs_sb[:, c:c + 1], in0=ss_sb[:, c:c + 1], in1=cst_sb[:, 0:1],
                                                      op=ALU.pow),
                 r=[("ss", c), "cst"], w=[("rs", c)])
            return rs_sb[:, c:c + 1], ("rs", c)

        def prenorm_stats(t):
            return rms_stats(x_sb[:, t, :], [("x", t)])

        def prenorm_apply(l, t, goff, mode, stat, band=None, first=True):
            parts = prenorm_parts(l, t, goff, mode, stat, band=band, first=first)
            parts["copy"]()
            parts["pe"]()
            parts["evac"]()

        def prenorm_parts(l, t, goff, mode, stat, band=None, first=True):
            rstd, rkey = stat
            hs = cnt["hb"] % 3
            hprev = (cnt["hb"] - 1) % 3
            cnt["hb"] += 1
            b0 = 4 + (cnt["hb"] % 2) * 2
            gain = pp(l, goff, 8)

            def copy():
                P.op("scalar", lambda e: e.activation(out=hb_sb[:, hs, :], in_=x_sb[:, t, :], func=AF.Copy, scale=rstd),
                     r=[("x", t), rkey], w=[("hb", hs)])

            if mode == "nat":
                pt = ps[:, b0, :].bitcast(BF16).rearrange("p (k t) -> p k t", k=8)

                def pe():
                    for k in range(8):
                        P.op("tensor", lambda e, k=k: e.transpose(out=pt[:, k, :], in_=hb_sb[:, hs, k * 128:(k + 1) * 128],
                                                                   identity=ident_b[:]),
                             r=[("hb", hs), "ident_b"], w=[("ps", b0)])

                def evac():
                    out = h_sb[:, :, 16 + t * 128:16 + (t + 1) * 128]
                    P.op("vector", lambda e: e.tensor_tensor(out=out, in0=pt, in1=gain.unsqueeze(2).to_broadcast([128, 8, 128]),
                                                              op=ALU.mult),
                         r=[("ps", b0), "pp"], w=[("ht", t)])
                return dict(copy=copy, pe=pe, evac=evac)

            pt = ps[:, b0:b0 + 2, :].rearrange("p b (k t) -> p (b k) t", k=4)
            pkeys = [("ps", b0), ("ps", b0 + 1)]
            if mode == "deint":
                def pe():
                    for k in range(8):
                        P.op("tensor", lambda e, k=k: e.matmul(pt[:, k, :], hb_sb[:, hs, k * 128:(k + 1) * 128], perm_b[:],
                                                                start=True, stop=True),
                             r=[("hb", hs), "perm_b"], w=[("ps", b0 + k // 4)])

                def evac():
                    out = mkap(h_sb[:], 16 + t * 16, [[HW, 8], [256, 8], [1, 16]])
                    in0 = pt.rearrange("p k (s c) -> p k s c", s=8)
                    g4 = mkap(gain, 0, [[1, 8], [0, 8], [0, 16]])
                    P.op("vector", lambda e: e.tensor_tensor(out=out, in0=in0, in1=g4, op=ALU.mult),
                         r=pkeys + ["pp"], w=HT + HK)
                return dict(copy=copy, pe=pe, evac=evac)

            def pe():
                for k in range(8):
                    g = k // 2
                    bm = band[:, (8 + g if first else g), :]
                    P.op("tensor", lambda e, k=k, bm=bm: e.matmul(pt[:, k, :], hb_sb[:, hs, k * 128:(k + 1) * 128], bm,
                                                                   start=True, stop=first),
                         r=[("hb", hs), "band"], w=[("ps", b0 + k // 4)])
                    if not first:
                        bp = band[:, 4 + g, :]
                        P.op("tensor", lambda e, k=k, bp=bp: e.matmul(pt[:, k, :], hb_sb[:, hprev, k * 128:(k + 1) * 128], bp,
                                                                       start=False, stop=True),
                             r=[("hb", hprev), "band"], w=[("ps", b0 + k // 4)])

            def evac():
                out = h_sb[:, :, 16 + t * 128:16 + (t + 1) * 128]
                if t % 2 == 0:
                    P.op("vector", lambda e: e.tensor_copy(out=out, in_=pt), r=pkeys, w=[("ht", t)])
                else:
                    P.op("scalar", lambda e: e.activation(out=out, in_=pt, func=AF.Copy), r=pkeys, w=[("ht", t)])
            return dict(copy=copy, pe=pe, evac=evac)

        def prenorm_many(l, tiles, goff, mode, band=None, early=None):
            early = early or {}
            gs_ = lambda t: early.pop(t) if t in early else prenorm_stats(t)
            st_next = gs_(tiles[0])
            for i, t in enumerate(tiles):
                st_cur = st_next
                if i + 1 < len(tiles):
                    st_next = gs_(tiles[i + 1])
                prenorm_apply(l, t, goff, mode, st_cur, band=band, first=(t == 0))

        def post_stats(src_ap, src_keys):
            return rms_stats(src_ap, src_keys)

        def post_apply(which, src_ap, src_keys, t, stat):
            rstd, rkey = stat
            P.op("vector", lambda e: e.tensor_tensor(out=tmp_sb[:, 0, :], in0=src_ap, in1=gpost_sb[:, which, :], op=ALU.mult),
                 r=list(src_keys) + [("gpost", which)], w=["tmp"])
            P.op("vector", lambda e: e.scalar_tensor_tensor(out=x_sb[:, t, :], in0=tmp_sb[:, 0, :], scalar=rstd,
                                                             in1=x_sb[:, t, :], op0=ALU.mult, op1=ALU.add),
                 r=["tmp", rkey, ("x", t)], w=[("x", t)])

        def post_update(l, which, src_ap, src_keys, t):
            post_apply(which, src_ap, src_keys, t, post_stats(src_ap, src_keys))

        def load_gpost(l, which):
            P.op("sync", lambda e: e.dma_start(out=gpost_sb[:, which, :], in_=gpost_d[l, which:which + 1, :].partition_broadcast(128)),
                 w=[("gpost", which)], dma=("gp", which))

        NSLOT = 3
        wd_sb = carve(0, [NF, D], BF16)
        wgv_sb = carve(45056, [NSLOT, 2048], BF16)
        hdn_sb = carve(57344, [NF, 512], BF16)

        def prefetch_wgv(l, extra_w=()):
            base = cnt["gv"]
            for i in range(NSLOT):
                slot = (base + i) % NSLOT
                P.op("sync", lambda e, slot=slot, i=i: e.dma_start(out=wgv_sb[:, slot, :], in_=wffn_b[l, i, :, 0:2048]),
                     r=[("wffn_b", l, ff) for ff in range(NF)], w=[("wgv", slot)] + list(extra_w), dma=("wgv", slot))
            return NSLOT

        def load_wd(l, tok=True):
            for hf in range(2):
                fs = slice(hf * 11, hf * 11 + 11)
                (OPA if tok else P.op)("sync", lambda e, fs=fs: e.dma_start(
                    out=wd_sb[:, fs, :], in_=wffn_b[l, fs, :, 2048:3072].rearrange("f p d -> p f d")),
                    r=[("wffn_b", l, f) for f in range(NF)], w=[("wd", hf)], dma=("wd", hf))

        def ffn(l, seq=0, last=False, wd_loaded=False, early=None, wgv_pre=0, pre0_done=False):
            load_gpost(l, 1)
            if not pre0_done:
                prenorm_many(l, [0, 1, 2, 3], 8, "nat")
            phase_enter()
            cw = 16
            cb = 16 + 66
            pending_io = []

            pending_ld = []

            def flush_st():
                for t in pending_io:
                    P.op("sync", lambda e, t=t: e.dma_start(out=y_d[seq, t * 128:(t + 1) * 128, :], in_=x_sb[:, t, :]),
                         r=[("x", t)], w=[("xout", seq, t)], dma=("xst", t))
                    pending_ld.append(t)
                del pending_io[:]

            def flush_ld():
                for t in pending_ld:
                    if seq + 1 < nseq:
                        P.op("sync", lambda e, t=t: e.dma_start(out=x_sb[:, t, :], in_=x_d[seq + 1, t * 128:(t + 1) * 128, :]),
                             w=[("x", t)], dma=("xld", t))
                del pending_ld[:]

            for tb in range(NTB):
                hkeys = [("ht", t) for t in range(tb * 4, tb * 4 + 4)]
                c0 = 16 + tb * 512
                nstat = {}
                nparts = {}
                for f in range(NF):
                    if tb + 1 < NTB and f in (0, 5, 10, 15):
                        tn = (tb + 1) * 4 + (0, 5, 10, 15).index(f)
                        nstat[tn] = prenorm_stats(tn)
                    if tb + 1 < NTB and f in (2, 7, 12, 17):
                        tn = (tb + 1) * 4 + (2, 7, 12, 17).index(f)
                        nparts[tn] = prenorm_parts(l, tn, 8, "nat", nstat[tn])
                        nparts[tn]["copy"]()
                    if tb + 1 < NTB and f in (4, 9, 14, 19):
                        tn = (tb + 1) * 4 + (4, 9, 14, 19).index(f)
                        nparts[tn]["pe"]()
                        nparts[tn]["evac"]()
                    if tb == 0 and f == 6 and not wd_loaded:
                        load_wd(l)
                    if f == 5 and pending_io:
                        flush_st()
                    if f == 14 and pending_ld:
                        flush_ld()
                    slot = cnt["gv"] % NSLOT
                    gs = cnt["gv"] % 2
                    cnt["gv"] += 1
                    if not (tb == 0 and f < wgv_pre):
                        OPA("sync", lambda e, slot=slot, f=f: e.dma_start(out=wgv_sb[:, slot, :], in_=wffn_b[l, f, :, 0:2048]),
                            r=[("wffn_b", l, ff) for ff in range(NF)], w=[("wgv", slot)], dma=("wgv", slot))
                    bg, bv = gs * 2, gs * 2 + 1
                    for (bank, woff) in ((bg, 0), (bv, 1024)):
                        for k in range(8):
                            OPA("tensor", lambda e, k=k, slot=slot, bank=bank, woff=woff, c0=c0: e.matmul(
                                ps[:, bank, :], wgv_sb[:, slot, woff + k * 128:woff + (k + 1) * 128], h_sb[:, k, c0:c0 + 512],
                                start=(k == 0), stop=(k == 7)),
                                r=[("wgv", slot)] + hkeys, w=[("ps", bank)])
                    ts = cnt["t"] % 2
                    cnt["t"] += 1
                    g_ps = ps[:, bg, :]
                    v_ps = ps[:, bv, :]
                    w0, w1, w2 = pp(l, cw + 0 * 22 + f), pp(l, cw + 1 * 22 + f), pp(l, cw + 2 * 22 + f)
                    bb = pp(l, cb + f)
                    tk = ("t0", ts)
                    P.op("scalar", lambda e, ts=ts, g_ps=g_ps, w2=w2, bb=bb: e.activation(
                        out=t0_sb[:, ts, :], in_=g_ps, func=AF.Identity, scale=w2, bias=bb),
                        r=[("ps", bg), "pp"], w=[tk])
                    if tb > 0:
                        P.op("vector", lambda e, ts=ts, f=f, w1=w1: e.scalar_tensor_tensor(
                            out=t0_sb[:, ts, 0:1], in0=halo_sb[:, f, 1:2], scalar=w1, in1=t0_sb[:, ts, 0:1],
                            op0=ALU.mult, op1=ALU.add), r=[("halo", f), tk, "pp"], w=[tk])
                    P.op("vector", lambda e, ts=ts, g_ps=g_ps, w1=w1: e.scalar_tensor_tensor(
                        out=t0_sb[:, ts, 1:512], in0=g_ps[:, 0:511], scalar=w1, in1=t0_sb[:, ts, 1:512],
                        op0=ALU.mult, op1=ALU.add), r=[("ps", bg), tk, "pp"], w=[tk])
                    P.op("vector", lambda e, ts=ts, g_ps=g_ps, w0=w0: e.scalar_tensor_tensor(
                        out=t0_sb[:, ts, 2:512], in0=g_ps[:, 0:510], scalar=w0, in1=t0_sb[:, ts, 2:512],
                        op0=ALU.mult, op1=ALU.add), r=[("ps", bg), tk, "pp"], w=[tk])
                    if tb > 0:
                        P.op("vector", lambda e, ts=ts, f=f, w0=w0: e.scalar_tensor_tensor(
                            out=t0_sb[:, ts, 0:2], in0=halo_sb[:, f, 0:2], scalar=w0, in1=t0_sb[:, ts, 0:2],
                            op0=ALU.mult, op1=ALU.add), r=[("halo", f), tk, "pp"], w=[tk])
                    if tb < NTB - 1:
                        P.op("vector", lambda e, f=f, g_ps=g_ps: e.tensor_copy(out=halo_sb[:, f, :], in_=g_ps[:, 510:512]),
                             r=[("ps", bg)], w=[("halo", f)])
                    P.op("scalar", lambda e, ts=ts: e.activation(out=t0_sb[:, ts, :], in_=t0_sb[:, ts, :],
                                                                  func=AF.Gelu_apprx_tanh), r=[tk], w=[tk])
                    OPA("vector", lambda e, ts=ts, f=f, v_ps=v_ps: e.tensor_tensor(
                        out=hdn_sb[:, f, :], in0=t0_sb[:, ts, :], in1=v_ps, op=ALU.mult),
                        r=[tk, ("ps", bv)], w=[("hdn", f)])
                for tt in range(4):
                    t = tb * 4 + tt
                    ds = cnt["dn"] % 2
                    cnt["dn"] += 1
                    b0 = 4 + ds * 2
                    for f in range(NF):
                        for dh in range(2):
                            OPA("tensor", lambda e, f=f, dh=dh, tt=tt, b0=b0: e.matmul(
                                ps[:, b0 + dh, :], hdn_sb[:, f, tt * 128:(tt + 1) * 128],
                                wd_sb[:, f, dh * 512:(dh + 1) * 512], start=(f == 0), stop=(f == NF - 1)),
                                r=[("hdn", f), ("wd", f // 11)], w=[("ps", b0 + dh)])
                    src = ps[:, b0:b0 + 2, :].rearrange("p a b -> p (a b)")
                    post_update(l, 1, src, [("ps", b0), ("ps", b0 + 1)], t)
                    if last:
                        pending_io.append(t)
                    if early is not None and t >= 1:
                        early[t - 1] = prenorm_stats(t - 1)
            if last:
                flush_st()
                flush_ld()
            if early is not None:
                early[NT - 1] = prenorm_stats(NT - 1)

        def pool_prep(j):
            phase_enter()
            T = arena[:, 0:4096].bitcast(F32)
            SC = arena[:, 4096:6144].bitcast(F32)
            STG = arena[:, 6144:8192]
            OPA("sync", lambda e: e.dma_start(out=T, in_=poolw_d[j]), w=["pp_T"], dma="pp_T")
            OPA("sync", lambda e: e.dma_start(out=SC, in_=pscale_d[j:j + 1, :].partition_broadcast(128)), w=["pp_S"], dma="pp_S")
            lpool = 2 * j + 1
            for kch in range(8):
                OPA("vector", lambda e, kch=kch: e.tensor_scalar(
                    out=T[:, kch * 256:(kch + 1) * 256], in0=T[:, kch * 256:(kch + 1) * 256],
                    scalar1=pp(lpool, 0, 8)[:, kch:kch + 1], scalar2=None, op0=ALU.mult), r=["pp_T", "pp"], w=["pp_T"])
            OPA("vector", lambda e: e.tensor_tensor(
                out=STG.rearrange("p (g k n) -> p g k n", g=4, k=2), in0=T.rearrange("p (g k n) -> p g k n", g=4, k=2),
                in1=mkap(SC, 0, [[256, 4], [0, 2], [1, 256]]), op=ALU.mult), r=["pp_T", "pp_S"], w=["pp_G"])
            OPA("sync", lambda e: e.dma_start(out=poolw_b[j], in_=STG), r=["pp_G"], w=[("poolw_b", j)], dma="pp_O")

        pre_n = [0]

        def pool_mixer(l, prefetch_wd=False, early=None):
            j = l // 2
            load_gpost(l, 0)
            phase_enter()
            if prefetch_wd:
                load_wd(l, tok=False)
                pre_n[0] = prefetch_wgv(l)
            wp = arena[:, 57344 // 2:(57344 + 4096) // 2].rearrange("p (g k n) -> p g k n", g=4, k=2)
            band = arena[:, 61440 // 2:(61440 + 3072) // 2].rearrange("p (m t) -> p m t", m=12)
            OPA("sync", lambda e: e.dma_start(out=wp, in_=poolw_b[j].rearrange("p (g k n) -> p g k n", g=4, k=2)),
                r=[("poolw_b", j)], w=["wp"], dma="wp")
            OPA("gpsimd", lambda e: e.dma_start(out=band, in_=band_d.rearrange("p (m t) -> p m t", m=12)), w=["band"], dma="band")
            early = early or {}
            gs_ = lambda t: early.pop(t) if t in early else prenorm_stats(t)
            stats = {0: gs_(0)}
            outs = {}
            for i in range(NT + 2):
                parts = None
                if i < NT:
                    parts = prenorm_parts(l, i, 0, "pool", stats[i], band=band, first=(i == 0))
                    parts["copy"]()
                pst = None
                if i - 2 >= 0:
                    src, keys = outs[i - 2]
                    pst = post_stats(src, keys)
                if parts is not None:
                    parts["pe"]()
                if i - 2 >= 0:
                    src, keys = outs[i - 2]
                    post_apply(0, src, keys, i - 2, pst)
                if i + 1 < NT:
                    stats[i + 1] = gs_(i + 1)
                if parts is not None:
                    parts["evac"]()
                if 0 <= i - 1 < NT:
                    outs[i - 1] = pool_mm(wp, i - 1)

        def pool_mm(wp, t):
            bsel = cnt["glu"] % 2
            cnt["glu"] += 1
            b0 = bsel * 2
            for g in range(4):
                for kk in range(2):
                    OPA("tensor", lambda e, g=g, kk=kk: e.matmul(
                        ps[:, b0 + g // 2, (g % 2) * 256:(g % 2) * 256 + 256],
                        h_sb[:, 2 * g + kk, 16 + t * 128:16 + (t + 1) * 128], wp[:, g, kk, :],
                        start=(kk == 0), stop=(kk == 1)),
                        r=[("ht", t), "wp"], w=[("ps", b0 + g // 2)])
            return ps[:, b0:b0 + 2, :].rearrange("p a b -> p (a b)"), [("ps", b0), ("ps", b0 + 1)]

        def s5_prep(j, after_loads=None):
            phase_enter()
            big = lambda i: arena[:, i * 2048:(i + 1) * 2048].bitcast(F32)
            Br, Bi, Cr, Ci, Cin, BBr, BBi, Mr, Mi, T1, T2, T3, T4 = [big(i) for i in range(13)]
            o = 13 * 4096
            Kst = arena[:, o // 2:(o + 2048) // 2].rearrange("p (a b) -> p a b", a=8)
            Wzst = arena[:, (o + 2048) // 2:(o + 6144) // 2].rearrange("p (a r b) -> p a r b", a=8, r=2)
            Wcst = arena[:, (o + 6144) // 2:(o + 10240) // 2].rearrange("p (r b) -> p r b", r=2)
            o2 = o + 10240
            sm = lambda i: arena[:, (o2 + i * 128) // 2:(o2 + (i + 1) * 128) // 2].bitcast(F32)
            par = arena[:, (o2 + 40 * 128) // 2:(o2 + 40 * 128 + 384) // 2].bitcast(F32)
            pw = arena[:, (o2 + 44 * 128) // 2:(o2 + 44 * 128 + 2304) // 2].bitcast(F32).rearrange("p (k r q) -> p k r q", k=9, r=2)
            lr, li, ld = par[:, 0:32], par[:, 32:64], par[:, 64:96]
            dt, lrdt, ang, mag, sn, cs, abr, abi, den, nr, fr, fi, ta, tb_, kf, ang2 = [sm(i) for i in range(16)]
            ki = sm(16).bitcast(I32)
            V = lambda fn, r, w: OPA("vector", fn, r, w, strict=True)
            A_ = lambda fn, r, w: OPA("scalar", fn, r, w, strict=True)
            K1 = ["s5sm"]
            OPA("sync", lambda e: e.dma_start(out=par, in_=s5p_d[j]), w=K1, dma="s5p")
            for i, tl in enumerate((Br, Bi, Cr, Ci)):
                OPA("sync", lambda e, i=i, tl=tl: e.dma_start(out=tl, in_=s5bc_d[j, i]), w=[("s5big", i)], dma=("s5bc", i))
            if after_loads is not None:
                after_loads()
            A_(lambda e: e.activation(out=dt, in_=ld, func=AF.Exp), K1, K1)
            V(lambda e: e.tensor_tensor(out=lrdt, in0=lr, in1=dt, op=ALU.mult), K1, K1)
            V(lambda e: e.tensor_tensor(out=ang, in0=li, in1=dt, op=ALU.mult), K1, K1)
            A_(lambda e: e.activation(out=mag, in_=lrdt, func=AF.Exp), K1, K1)
            C1, C2 = 6.28125, 0.0019353071795864769

            def sincos(dst, shift):
                V(lambda e: e.tensor_scalar(out=ang2, in0=ang, scalar1=shift, scalar2=None, op0=ALU.add), K1, K1)
                V(lambda e: e.tensor_scalar(out=ki, in0=ang2, scalar1=0.15915494309189535, scalar2=None, op0=ALU.mult), K1, K1)
                V(lambda e: e.tensor_copy(out=kf, in_=ki), K1, K1)
                V(lambda e: e.scalar_tensor_tensor(out=ta, in0=kf, scalar=-C1, in1=ang2, op0=ALU.mult, op1=ALU.add), K1, K1)
                V(lambda e: e.scalar_tensor_tensor(out=ta, in0=kf, scalar=-C2, in1=ta, op0=ALU.mult, op1=ALU.add), K1, K1)
                V(lambda e: e.tensor_scalar(out=ta, in0=ta, scalar1=-3.1415925, scalar2=3.1415925, op0=ALU.max, op1=ALU.min), K1, K1)
                A_(lambda e: e.activation(out=dst, in_=ta, func=AF.Sin), K1, K1)
            sincos(sn, 0.0)
            sincos(cs, 1.5707963267948966)
            V(lambda e: e.tensor_tensor(out=abr, in0=mag, in1=cs, op=ALU.mult), K1, K1)
            V(lambda e: e.tensor_tensor(out=abi, in0=mag, in1=sn, op=ALU.mult), K1, K1)
            V(lambda e: e.tensor_tensor(out=den, in0=lr, in1=lr, op=ALU.mult), K1, K1)
            V(lambda e: e.tensor_tensor(out=ta, in0=li, in1=li, op=ALU.mult), K1, K1)
            V(lambda e: e.tensor_tensor(out=den, in0=den, in1=ta, op=ALU.add), K1, K1)
            V(lambda e: e.reciprocal(out=den, in_=den), K1, K1)
            V(lambda e: e.tensor_scalar(out=nr, in0=abr, scalar1=-1.0, scalar2=None, op0=ALU.add), K1, K1)
            V(lambda e: e.tensor_tensor(out=ta, in0=nr, in1=lr, op=ALU.mult), K1, K1)
            V(lambda e: e.tensor_tensor(out=tb_, in0=abi, in1=li, op=ALU.mult), K1, K1)
            V(lambda e: e.tensor_tensor(out=ta, in0=ta, in1=tb_, op=ALU.add), K1, K1)
            V(lambda e: e.tensor_tensor(out=fr, in0=ta, in1=den, op=ALU.mult), K1, K1)
            V(lambda e: e.tensor_tensor(out=ta, in0=abi, in1=lr, op=ALU.mult), K1, K1)
            V(lambda e: e.tensor_tensor(out=tb_, in0=nr, in1=li, op=ALU.mult), K1, K1)
            V(lambda e: e.tensor_tensor(out=ta, in0=ta, in1=tb_, op=ALU.subtract), K1, K1)
            V(lambda e: e.tensor_tensor(out=fi, in0=ta, in1=den, op=ALU.mult), K1, K1)
            V(lambda e: e.memset(pw[:, 0, 0, :], 1.0), K1, K1)
            V(lambda e: e.memset(pw[:, 0, 1, :], 0.0), K1, K1)
            for k in range(8):
                V(lambda e, k=k: e.tensor_tensor(out=ta, in0=pw[:, k, 0, :], in1=abr, op=ALU.mult), K1, K1)
                V(lambda e, k=k: e.tensor_tensor(out=tb_, in0=pw[:, k, 1, :], in1=abi, op=ALU.mult), K1, K1)
                V(lambda e, k=k: e.tensor_tensor(out=pw[:, k + 1, 0, :], in0=ta, in1=tb_, op=ALU.subtract), K1, K1)
                V(lambda e, k=k: e.tensor_tensor(out=ta, in0=pw[:, k, 0, :], in1=abi, op=ALU.mult), K1, K1)
                V(lambda e, k=k: e.tensor_tensor(out=tb_, in0=pw[:, k, 1, :], in1=abr, op=ALU.mult), K1, K1)
                V(lambda e, k=k: e.tensor_tensor(out=pw[:, k + 1, 1, :], in0=ta, in1=tb_, op=ALU.add), K1, K1)
            V(lambda e: e.tensor_copy(out=scoef_sb[:, j, 0, 0:32], in_=pw[:, 8, 0, :]), K1, [("scoef", j)])
            V(lambda e: e.tensor_copy(out=scoef_sb[:, j, 0, 32:64], in_=pw[:, 8, 0, :]), K1, [("scoef", j)])
            V(lambda e: e.tensor_scalar(out=scoef_sb[:, j, 1, 0:32], in0=pw[:, 8, 1, :], scalar1=-1.0, scalar2=None, op0=ALU.mult),
              K1, [("scoef", j)])
            V(lambda e: e.tensor_copy(out=scoef_sb[:, j, 1, 32:64], in_=pw[:, 8, 1, :]), K1, [("scoef", j)])

            v3 = lambda ap: ap.rearrange("p (q c) -> p q c", q=32)
            bc = lambda ap32: ap32.unsqueeze(2).to_broadcast([128, 32, 32])

            def cmul(out_r, out_i, ar, ai, Xr, Xi, rk, wk):
                V(lambda e: e.tensor_tensor(out=v3(T1), in0=v3(Xr), in1=bc(ar), op=ALU.mult), rk, ["T1"])
                V(lambda e: e.tensor_tensor(out=v3(T2), in0=v3(Xi), in1=bc(ai), op=ALU.mult), rk, ["T2"])
                V(lambda e: e.tensor_tensor(out=out_r, in0=T1, in1=T2, op=ALU.subtract), ["T1", "T2"], [wk[0]])
                V(lambda e: e.tensor_tensor(out=v3(T1), in0=v3(Xi), in1=bc(ar), op=ALU.mult), rk, ["T1"])
                V(lambda e: e.tensor_tensor(out=v3(T2), in0=v3(Xr), in1=bc(ai), op=ALU.mult), rk, ["T2"])
                V(lambda e: e.tensor_tensor(out=out_i, in0=T1, in1=T2, op=ALU.add), ["T1", "T2"], [wk[1]])

            cmul(BBr, BBi, fr, fi, Br, Bi, K1 + [("s5big", 0), ("s5big", 1)], ["BBr", "BBi"])
            V(lambda e: e.tensor_scalar(out=Cin, in0=Ci, scalar1=-1.0, scalar2=None, op0=ALU.mult), [("s5big", 3)], ["Cin"])
            G_ = lambda fn, r, w: OPA("vector", fn, r, w)
            for k in range(8):
                s_z = 7 - k
                cmul(Mr, Mi, pw[:, k, 0, :], pw[:, k, 1, :], BBr, BBi, K1 + ["BBr", "BBi"], ["Mr", "Mi"])
                for jb in range(2):
                    pb = (cnt["za"] % 2) * 3
                    cnt["za"] += 1
                    for jl in range(4):
                        jt = jb * 4 + jl
                        sl = slice(jt * 128, (jt + 1) * 128)
                        csl = slice(jl * 128, (jl + 1) * 128)
                        OPA("tensor", lambda e, sl=sl, csl=csl, pb=pb: e.matmul(ps[:, pb, csl], Mr[:, sl], Cr[:, sl], start=True, stop=False),
                            r=["Mr", ("s5big", 2)], w=[("ps", pb)])
                        OPA("tensor", lambda e, sl=sl, csl=csl, pb=pb: e.matmul(ps[:, pb, csl], Mi[:, sl], Cin[:, sl], start=False, stop=True),
                            r=["Mi", "Cin"], w=[("ps", pb)])
                    for jl in range(4):
                        jt = jb * 4 + jl
                        sl = slice(jt * 128, (jt + 1) * 128)
                        csl = slice(jl * 128, (jl + 1) * 128)
                        OPA("tensor", lambda e, sl=sl, csl=csl, pb=pb: e.transpose(out=ps[:, pb + 1, csl], in_=Mr[:, sl], identity=ident_f[:]),
                            r=["Mr", "ident_f"], w=[("ps", pb + 1)])
                    for jl in range(4):
                        jt = jb * 4 + jl
                        sl = slice(jt * 128, (jt + 1) * 128)
                        csl = slice(jl * 128, (jl + 1) * 128)
                        OPA("tensor", lambda e, sl=sl, csl=csl, pb=pb: e.transpose(out=ps[:, pb + 2, csl], in_=Mi[:, sl], identity=ident_f[:]),
                            r=["Mi", "ident_f"], w=[("ps", pb + 2)])
                    js = slice(jb * 4, jb * 4 + 4)
                    V(lambda e, js=js, pb=pb: e.tensor_tensor(out=Kst[:, js, :], in0=ps[:, pb, :].rearrange("p (a b) -> p a b", a=4),
                                                             in1=mkap(mask_sb[:], 0, [[0, 4], [1, 128]]), op=ALU.mult),
                      [("ps", pb), "mask"], ["Kst"])
                    A_(lambda e, js=js, pb=pb: e.activation(out=Wzst[:, js, 0, :], in_=ps[:, pb + 1, :].rearrange("p (a b) -> p a b", a=4),
                                                            func=AF.Copy), [("ps", pb + 1)], ["Wzst"])
                    A_(lambda e, js=js, pb=pb: e.activation(out=Wzst[:, js, 1, :], in_=ps[:, pb + 2, :].rearrange("p (a b) -> p a b", a=4),
                                                            func=AF.Copy), [("ps", pb + 2)], ["Wzst"])
                OPA("sync", lambda e, k=k: e.dma_start(out=s5w_b[j, :, :, 2048 + k * 128:2048 + (k + 1) * 128].rearrange("t p c -> p t c"), in_=Kst),
                    r=["Kst"], w=[("s5w_b", j, "k", k)], dma="s5k")
                OPA("sync", lambda e, s_z=s_z: e.dma_start(
                    out=s5w_b[j, :, :, s_z * 256:(s_z + 1) * 256].rearrange("t p (r c) -> p t r c", r=2), in_=Wzst),
                    r=["Wzst"], w=[("s5w_b", j, "z", k)], dma="s5z")
                G_(lambda e, k=k: e.tensor_tensor(out=v3(T3), in0=v3(Cr), in1=bc(pw[:, k + 1, 0, :]), op=ALU.mult), K1 + [("s5big", 2)], ["T3"])
                G_(lambda e, k=k: e.tensor_tensor(out=v3(T4), in0=v3(Ci), in1=bc(pw[:, k + 1, 1, :]), op=ALU.mult), K1 + [("s5big", 3)], ["T4"])
                G_(lambda e: e.tensor_tensor(out=Wcst[:, 0, :], in0=T3, in1=T4, op=ALU.subtract), ["T3", "T4"], ["Wcst"])
                G_(lambda e, k=k: e.tensor_tensor(out=v3(T3), in0=v3(Cr), in1=bc(pw[:, k + 1, 1, :]), op=ALU.mult), K1 + [("s5big", 2)], ["T3"])
                G_(lambda e, k=k: e.tensor_tensor(out=v3(T4), in0=v3(Ci), in1=bc(pw[:, k + 1, 0, :]), op=ALU.mult), K1 + [("s5big", 3)], ["T4"])
                G_(lambda e: e.tensor_tensor(out=T3, in0=T3, in1=T4, op=ALU.add), ["T3", "T4"], ["T3"])
                G_(lambda e: e.tensor_scalar(out=Wcst[:, 1, :], in0=T3, scalar1=-1.0, scalar2=None, op0=ALU.mult), ["T3"], ["Wcst"])
                for ri in range(2):
                    OPA("sync", lambda e, k=k, ri=ri: e.dma_start(
                        out=s5w_b[j, :, :, 3072 + k * 256 + ri * 128:3072 + k * 256 + (ri + 1) * 128].rearrange("t p c -> p t c"),
                        in_=Wcst[:, ri, :].rearrange("p (t c) -> p t c", t=8)),
                        r=["Wcst"], w=[("s5w_b", j, "c", k) if ri == 0 else ("s5w_b", j, "c2", k)], dma="s5c")

        S5W_KEYS = lambda j: [("s5w_b", j, a, k) for a in ("k", "z", "c", "c2") for k in range(8)]

        def s5_mixer(l, early=None, next_ffn=False, ffn_pre=None):
            j = l // 2
            load_gpost(l, 0)
            prenorm_many(l, list(range(NT)), 0, "deint", early=early)
            phase_enter()
            ZS = carve(0, [64, 256], BF16)
            ring = arena[:, 32768 // 2:(32768 + 8192) // 2].bitcast(F32).rearrange("p (s e) -> p s e", s=32)
            y32 = arena[:, 32768 // 2:(32768 + 8192) // 2].bitcast(F32)
            sg32 = arena[:, 32768 // 2:(32768 + 4096) // 2].bitcast(F32).rearrange("p (a b) -> p a b", a=2)
            bund = arena[:, 40960 // 2:(40960 + 20480) // 2].rearrange("p (a b) -> p a b", a=2)
            wgl = arena[:, 61440 // 2:(61440 + 16384) // 2].rearrange("p (k d) -> p k d", k=8)
            tA = arena[:, 77824 // 2:(77824 + 256) // 2].bitcast(F32)
            tB = arena[:, 78080 // 2:(78080 + 256) // 2].bitcast(F32)
            mfeat = arena[:, 0:16384].bitcast(F32).rearrange("p (a m b) -> p a m b", a=2, m=8)
            RK = [("ring", 0), ("ring", 1)]
            ZK = [("zs", q) for q in range(32)]
            dsk = pp_sb[:, PP_S5 + j * 16: PP_S5 + j * 16 + 8]
            bgl = pp_sb[:, PP_S5 + j * 16 + 8: PP_S5 + j * 16 + 16]
            def load_wgl():
                OPA("sync", lambda e: e.dma_start(out=wgl, in_=wglu_b[j].rearrange("p (k d) -> p k d", k=8)),
                    r=[("wglu_b", j)], w=["wgl"], dma="wgl")

            def load_bund(jt, part):
                slot = cnt["bund"] % 2
                cnt["bund"] += 1
                c0_, c1_ = (0, 2048) if part == "A" else (2048, 5120)
                OPA("sync", lambda e: e.dma_start(out=bund[:, slot, c0_:c1_], in_=s5w_b[j, jt, :, c0_:c1_]),
                    r=S5W_KEYS(j), w=[("bund", slot)], dma=("bund", slot))
                return slot

            for jt in range(8):
                slot = load_bund(jt, "A")
                pb = (jt % 2) * 4
                for ri in range(2):
                    for s in range(8):
                        wo = s * 256 + ri * 128
                        for ql in range(4):
                            OPA("tensor", lambda e, ql=ql, ri=ri, s=s, wo=wo, pb=pb, jt=jt, slot=slot: e.matmul(
                                ps[:, pb + ql, ri * 256:(ri + 1) * 256], bund[32 * ql:32 * ql + 32, slot, wo:wo + 128],
                                h_sb[32 * ql:32 * ql + 32, jt, 16 + s * 256:16 + (s + 1) * 256],
                                start=(s == 0), stop=(s == 7), tile_position=(32 * ql, 0)),
                                r=[("bund", slot), ("hk", jt)], w=[("ps", pb + ql)])
                for ql in range(4):
                    q = 4 * jt + ql
                    src = ps[:, pb + ql, :].rearrange("p (r c) -> p r c", r=2)
                    dst = mkap(ZS, q * 256, [[32 * 256, 2], [1, 256]])
                    if q % 2 == 0:
                        OPA("scalar", lambda e, src=src, dst=dst: e.activation(out=dst, in_=src, func=AF.Copy),
                            r=[("ps", pb + ql)], w=[("zs", q)])
                    else:
                        OPA("vector", lambda e, src=src, dst=dst: e.tensor_copy(out=dst, in_=src),
                            r=[("ps", pb + ql)], w=[("zs", q)])

            if DEBUG:
                OPA("sync", lambda e: e.dma_start(out=dbg_w, in_=s5w_b[j]), r=S5W_KEYS(j), w=["dbgout"], dma="dbg")
                OPA("sync", lambda e: e.dma_start(out=dbg_sc, in_=scoef_sb[:].rearrange("p a b c -> p (a b c)")), r=[("scoef", j)], w=["dbgout"], dma="dbg")
                OPA("sync", lambda e: e.dma_start(out=dbg_z, in_=arena[:, 0:16384]), r=ZK, w=["dbgout", "dbgz"], dma="dbg")
                OPA("sync", lambda e: e.dma_start(out=dbg_u, in_=h_sb[:].rearrange("p a b -> p (a b)")), r=HK, w=["dbgout"], dma="dbg")
                ZK = ZK + ["dbgz"]
            A1 = scoef_sb[:, j, 0, :]
            A2 = scoef_sb[:, j, 1, :]
            SE = scan_eng
            zcol = lambda c: mkap(ZS, c, [[256, 64]])
            OPA(SE, lambda e: e.tensor_copy(out=ring[:, 0, :], in_=zcol(0)), r=ZK + RK, w=[("ring", 0)])
            for c in range(1, 256):
                cur = ring[:, (c - 1) % 32, :]
                nxt = ring[:, c % 32, :]
                swp = mkap(cur, 32, [[-32, 2], [1, 32]])
                hk = ("ring", (c % 32) // 16)
                OPA(SE, lambda e, cur=cur: e.tensor_tensor(out=tA, in0=cur, in1=A1, op=ALU.mult), r=[("scoef", j)], w=["tA"])
                OPA(SE, lambda e, swp=swp: e.tensor_tensor(out=tB.rearrange("p (a b) -> p a b", a=2), in0=swp,
                                                           in1=A2.rearrange("p (a b) -> p a b", a=2), op=ALU.mult),
                    r=[("scoef", j)], w=["tB"])
                OPA(SE, lambda e: e.tensor_tensor(out=tA, in0=tA, in1=tB, op=ALU.add), r=["tB"], w=["tA"])
                OPA(SE, lambda e, nxt=nxt, c=c: e.tensor_tensor(out=nxt, in0=tA, in1=zcol(c), op=ALU.add), r=["tA"], w=[hk])
                if c % 16 == 15:
                    b = c // 16
                    half = b % 2
                    src = ring[:, half * 16:half * 16 + 16, :].rearrange("p s e -> p e s")
                    dst = ZS[:, :, b * 16:b * 16 + 16]
                    OPA("scalar", lambda e, src=src, dst=dst: e.activation(out=dst, in_=src, func=AF.Copy),
                        r=[("ring", half)], w=["zs_conv"])

            if DEBUG:
                OPA("sync", lambda e: e.dma_start(out=dbg_s, in_=arena[:, 0:16384]), r=["zs_conv"], w=["dbgout"], dma="dbg")
            for jt in range(8):
                slot = load_bund(jt, "C")
                if jt == 1:
                    load_wgl()
                yb = (cnt["y"] % 2) * 4
                cnt["y"] += 1
                Y = ps[:, yb:yb + 4, :].rearrange("p b (h c) -> p (b h) c", h=2)
                ykeys = [("ps", yb + i) for i in range(4)]
                for s in range(8):
                    for k in range(s + 1):
                        OPA("tensor", lambda e, s=s, k=k, jt=jt, Y=Y, slot=slot: e.matmul(
                            Y[:, s, :], bund[:, slot, 2048 + k * 128:2048 + (k + 1) * 128],
                            h_sb[:, jt, 16 + (s - k) * 256:16 + (s - k + 1) * 256], start=(k == 0), stop=False),
                            r=[("bund", slot), ("hk", jt)], w=[("ps", yb + s // 2)])
                    for ri in range(2):
                        for ql in range(4):
                            q = 4 * jt + ql
                            wo = 3072 + s * 256 + ri * 128 + ql * 32
                            last = (ql == 3 and ri == 1)
                            OPA("tensor", lambda e, s=s, ql=ql, ri=ri, q=q, wo=wo, last=last, Y=Y, slot=slot: e.matmul(
                                Y[32 * ql:32 * ql + 32, s, 1:256], bund[:, slot, wo:wo + 32], ZS[:, ri * 32 + q, 0:255],
                                start=False, stop=last, tile_position=(0, 32 * ql), skip_group_check=True),
                                r=[("bund", slot), "zs_conv"], w=[("ps", yb + s // 2)])
                y32v = mkap(y32, 0, [[1, 8], [8, 256]])
                u_de = h_sb[:, jt, 16:HW].rearrange("p (s c) -> p s c", s=8)
                OPA("vector", lambda e, jt=jt, Y=Y, y32v=y32v, u_de=u_de: e.scalar_tensor_tensor(
                    out=y32v, in0=u_de, scalar=dsk[:, jt:jt + 1], in1=Y, op0=ALU.mult, op1=ALU.add),
                    r=ykeys + [("hk", jt), "pp"], w=RK + ["y32"])
                OPA("scalar", lambda e, jt=jt: e.activation(out=h_sb[:, jt, 16:HW], in_=y32, func=AF.Gelu_apprx_tanh),
                    r=["y32"] + RK, w=[("yg", jt), ("hk", jt)])

            if DEBUG:
                OPA("sync", lambda e: e.dma_start(out=dbg_h, in_=h_sb[:].rearrange("p a b -> p (a b)")), r=[("yg", k) for k in range(8)], w=["dbgout"], dma="dbg")
            YG = [("yg", k) for k in range(8)]

            def glu_mm_one(tb, mslot, m):
                c0 = 16 + tb * 512
                hkb = [("ht", t) for t in range(tb * 4, tb * 4 + 4)]
                bank = cnt["glu"] % 4
                cnt["glu"] += 1
                for k in range(8):
                    OPA("tensor", lambda e, k=k: e.matmul(
                        ps[:, bank, :], wgl[:, k, m * 128:(m + 1) * 128], h_sb[:, k, c0:c0 + 512],
                        start=(k == 0), stop=(k == 7)), r=["wgl"] + YG + hkb, w=[("ps", bank)])
                ss_ = cnt["sg"] % 2
                cnt["sg"] += 1
                OPA("scalar", lambda e: e.activation(
                    out=sg32[:, ss_, :], in_=ps[:, bank, :], func=AF.Sigmoid, bias=bgl[:, m:m + 1]),
                    r=[("ps", bank), "pp"], w=[("sg", ss_)] + RK)
                OPA("vector", lambda e: e.tensor_tensor(
                    out=mfeat[:, mslot, m, :], in0=h_sb[:, m, c0:c0 + 512], in1=sg32[:, ss_, :], op=ALU.mult),
                    r=[("sg", ss_), ("yg", m)] + hkb, w=[("mf", mslot), "zs_conv"])

            def glu_out_tile(tb, mslot, tt):
                t = tb * 4 + tt
                ds = cnt["dn"] % 2
                cnt["dn"] += 1
                b0 = 4 + ds * 2
                for m in range(8):
                    OPA("tensor", lambda e, m=m: e.transpose(
                        out=ps[:, b0 + m // 4, (m % 4) * 128:(m % 4) * 128 + 128],
                        in_=mfeat[:, mslot, m, tt * 128:(tt + 1) * 128], identity=ident_f[:]),
                        r=[("mf", mslot), "ident_f"], w=[("ps", b0 + m // 4)])
                src = ps[:, b0:b0 + 2, :].rearrange("p a b -> p (a b)")
                post_update(l, 0, src, [("ps", b0), ("ps", b0 + 1)], t)

            if next_ffn:
                pre_n[0] = prefetch_wgv(l, extra_w=[("bund", 0), ("bund", 1)])
            for tb in range(NTB):
                for m in range(8):
                    glu_mm_one(tb, tb % 2, m)
                    if tb > 0 and m % 2 == 1:
                        glu_out_tile(tb - 1, (tb - 1) % 2, m // 2)
                    if tb == 1 and m == 7 and ffn_pre is not None:
                        ffn_pre()
            for tt in range(4):
                glu_out_tile(NTB - 1, (NTB - 1) % 2, tt)

        s5_js = sorted({l // 2 for l in layers if l % 2 == 0}) if do_mixer else []
        pool_js = sorted({l // 2 for l in layers if l % 2 == 1}) if do_mixer else []
        def first_x_loads():
            for t in range(NT):
                P.op("sync", lambda e, t=t: e.dma_start(out=x_sb[:, t, :], in_=x_d[0, t * 128:(t + 1) * 128, :]),
                     w=[("x", t)], dma=("xld", t))

        def bulk_casts():
            gate = [("s5w_b", s5_js[-1], "c2", 7)] if s5_js else []
            for j in s5_js:
                P.op("gpsimd", lambda e, j=j: e.dma_start(out=wglu_b[j], in_=wglu_d[j]), r=gate, w=[("wglu_b", j)], dma=("castg", j))
            if do_ffn:
                cast_layer(layers[0])

        if s5_js:
            for i, j in enumerate(s5_js):
                s5_prep(j, after_loads=first_x_loads if i == 0 else None)
        else:
            first_x_loads()
        for j in pool_js:
            pool_prep(j)
        bulk_casts()
        for s in range(nseq):
            if s > 0 and not do_ffn:
                for t in range(NT):
                    P.op("sync", lambda e, s=s, t=t: e.dma_start(out=x_sb[:, t, :], in_=x_d[s, t * 128:(t + 1) * 128, :]),
                         w=[("x", t)], dma=("xld", t))
            early = None
            for li, l in enumerate(layers):
                if s == 0 and li + 1 < nl and do_ffn:
                    cast_layer(layers[li + 1])
                pre = False
                p0 = False
                pre_n[0] = 0
                if do_mixer:
                    if l % 2 == 0:
                        s5_mixer(l, early=early, next_ffn=do_ffn)
                    else:
                        pre = do_ffn
                        pool_mixer(l, prefetch_wd=pre, early=early)
                early = {} if (do_mixer and do_ffn and li + 1 < nl) else None
                if do_ffn:
                    ffn(l, seq=s, last=(li == nl - 1), wd_loaded=pre, early=early, wgv_pre=pre_n[0], pre0_done=p0)
            if not do_ffn:
                for t in range(NT):
                    P.op("sync", lambda e, s=s, t=t: e.dma_start(out=y_d[s, t * 128:(t + 1) * 128, :], in_=x_sb[:, t, :]),
                         r=[("x", t)], w=[("xout", s, t)], dma=("xst", t))
        P.op("sync", None, r=[("xout", s, t) for s in range(nseq) for t in range(NT)] + (["dbgout"] if DEBUG else []))
        P.emit(nc, st)
    return nc


def host_layout(inp):
    f32 = np.float32
    g = lambda k: np.asarray(inp[k], f32)
    wg = g("ffn_w_gate").reshape(DEPTH, 8, 128, NF, 128).transpose(0, 3, 2, 1, 4).reshape(DEPTH, NF, 128, 1024)
    wv = g("ffn_w_val").reshape(DEPTH, 8, 128, NF, 128).transpose(0, 3, 2, 1, 4).reshape(DEPTH, NF, 128, 1024)
    wd = g("ffn_w_down").reshape(DEPTH, NF, 128, 1024)
    wffn = np.ascontiguousarray(np.concatenate([wg, wv, wd], axis=3))
    cols = []
    for l in range(DEPTH):
        cols.append(g("norm_mix_pre")[l].reshape(8, 128).T)
        cols.append(g("norm_ffn_pre")[l].reshape(8, 128).T)
        cw = g("ffn_conv_w")[l].reshape(3, NF, 128)
        cols.append(cw.transpose(2, 0, 1).reshape(128, 66))
        cols.append(g("ffn_conv_b")[l].reshape(NF, 128).T)
    for j in range(2):
        cols.append(g("s5_d")[j].reshape(8, 128).T)
        cols.append(g("s5_b_glu")[j].reshape(8, 128).T)
    pp = np.ascontiguousarray(np.concatenate(cols, axis=1))
    gpost = np.ascontiguousarray(np.stack([g("norm_mix_post"), g("norm_ffn_post")], axis=1))
    ident = np.eye(128, dtype=f32)
    mask = np.kron(np.eye(8, dtype=f32), np.ones((16, 16), f32))
    perm = np.zeros((128, 128), f32)
    for i in range(128):
        perm[i, (i % 8) * 16 + i // 8] = 1.0
    band = np.zeros((12, 128, 128), f32)
    for gi, win in enumerate((2, 4, 8, 16)):
        for t in range(128):
            for jj in range(win):
                tp = t - jj
                if tp >= 0:
                    band[gi, tp, t] += 1.0 / win
                    band[8 + gi, tp, t] += 1.0 / min(t + 1, win)
                else:
                    band[4 + gi, 128 + tp, t] += 1.0 / win
            band[gi, t, t] -= 1.0
            band[8 + gi, t, t] -= 1.0
    band = np.ascontiguousarray(band.transpose(1, 0, 2).reshape(128, 12 * 128))
    poolw = np.ascontiguousarray(g("pool_w").reshape(2, 4, 2, 128, 256).transpose(0, 3, 1, 2, 4).reshape(2, 128, 2048))
    pscale = g("pool_scale")
    def gp(a):
        return a.reshape(2, 32, 2, 64).transpose(0, 2, 3, 1).reshape(2, 128, 32)
    ld = np.broadcast_to(g("s5_log_dt").reshape(2, 32, 2).transpose(0, 2, 1)[:, :, None, :], (2, 2, 64, 32)).reshape(2, 128, 32)
    s5p = np.ascontiguousarray(np.concatenate([gp(g("s5_lambda_re")), gp(g("s5_lambda_im")), ld], axis=2))
    def padB(b):
        out = np.zeros((2, 2, 64, 32, 2, 16), f32)
        bb = b.reshape(2, 32, 2, 64, 16)
        for par in range(2):
            out[:, par, :, :, par, :] = bb[:, :, par].transpose(0, 2, 1, 3)
        return out.reshape(2, 128, 1024)
    def padC(c):
        return padB(c.transpose(0, 1, 3, 2))
    s5bc = np.ascontiguousarray(np.stack([padB(g("s5_b_re")), padB(g("s5_b_im")), padC(g("s5_c_re")), padC(g("s5_c_im"))], axis=1))
    wglu = np.ascontiguousarray(g("s5_w_glu").reshape(2, 8, 128, 1024).transpose(0, 2, 1, 3).reshape(2, 128, 8192))
    return dict(wffn=wffn, pp=pp, gpost=gpost, ident=ident, mask=mask, perm=perm, band=band, poolw=poolw, pscale=pscale,
                s5p=s5p, s5bc=s5bc, wglu=wglu)


def kernel(**inputs):
    x = np.asarray(inputs["x"], np.float32)
    shared = host_layout(inputs)
    nseq = x.shape[0] // NCORES
    nc = build(nseq=nseq)
    in_maps = [dict(shared, x=np.ascontiguousarray(x[c * nseq:(c + 1) * nseq])) for c in range(NCORES)]
    res = run_bass_kernel_spmd(nc, in_maps, core_ids=list(range(NCORES)))
    return np.concatenate([r["y"] for r in res.results], axis=0)
```
